# Optimizing a Trainium2 kernel written in Bass

```python
import jax, jax.numpy as jnp
from jax import lax
import numpy as np

D_MODEL = 1024
BATCH = 4
SEQ = 8192
DEPTH = 1

GRID_W = 64
NA_HEADS = 8
NA_HEAD_DIM = 64
D_ATTN = NA_HEADS * NA_HEAD_DIM
NA_KH = 8
NA_KW = 16
Q_BLOCK_W = 16
K_BLOCK_W = 2 * NA_KW
N_COL_BLOCKS = GRID_W // Q_BLOCK_W
CONV_CH = 512
CONV_K = 31
D_FF = 2816
FFN_CONV_K = 3
D_IN = 3 * D_ATTN + 2 * CONV_CH + 2 * D_MODEL
EPS = 1e-6
NEG = -1e30

kernel_name = "hybrid_natten_conformer_convffn_block"


def _rmsnorm(x, g):
    xf = x.astype(jnp.float32)
    y = xf * lax.rsqrt(jnp.mean(xf * xf, axis=-1, keepdims=True) + EPS)
    return (y * g.astype(jnp.float32)).astype(x.dtype)


def _layernorm(x, g, b):
    xf = x.astype(jnp.float32)
    mu = jnp.mean(xf, axis=-1, keepdims=True)
    var = jnp.mean(jnp.square(xf - mu), axis=-1, keepdims=True)
    y = (xf - mu) * lax.rsqrt(var + EPS)
    return (y * g.astype(jnp.float32) + b.astype(jnp.float32)).astype(x.dtype)


def _dwconv_centred(x, w, b):
    k = w.shape[0]
    pad = k // 2
    y = lax.conv_general_dilated(
        x, w[:, None, :].astype(x.dtype), window_strides=(1,), padding=[(pad, pad)],
        dimension_numbers=("NWC", "WIO", "NWC"), feature_group_count=x.shape[-1])
    return y + b.astype(x.dtype)


def _neighbourhood_attention(q, k, v, rpb):
    bsz, s, h, dh = q.shape
    rows = s // GRID_W
    kh = min(NA_KH, rows)
    scale = NA_HEAD_DIM ** -0.5
    qg = q.reshape(bsz, rows, GRID_W, h, dh)
    kg = k.reshape(bsz, rows, GRID_W, h, dh)
    vg = v.reshape(bsz, rows, GRID_W, h, dh)

    col = jnp.arange(GRID_W)
    col_start = jnp.clip(col - NA_KW // 2, 0, GRID_W - NA_KW)
    blk_start = jnp.minimum(col_start[jnp.arange(N_COL_BLOCKS) * Q_BLOCK_W],
                            GRID_W - K_BLOCK_W)
    key_col = blk_start[:, None] + jnp.arange(K_BLOCK_W)
    q_col = col.reshape(N_COL_BLOCKS, Q_BLOCK_W)
    cs = col_start.reshape(N_COL_BLOCKS, Q_BLOCK_W)
    kc = key_col[:, None, :]
    in_win = (kc >= cs[..., None]) & (kc < cs[..., None] + NA_KW)
    col_off = jnp.clip(kc - q_col[..., None] + NA_KW - 1, 0, 2 * NA_KW - 2)
    rpb_col = rpb.astype(jnp.float32)[:, :, col_off]
    mask = in_win[None, None, :, :, None, :]

    def one_row(r):
        rs = jnp.clip(r - kh // 2, 0, rows - kh)
        q_r = lax.dynamic_index_in_dim(qg, r, axis=1, keepdims=False)
        q_r = q_r.reshape(bsz, N_COL_BLOCKS, Q_BLOCK_W, h, dh)
        k_r = lax.dynamic_slice_in_dim(kg, rs, kh, axis=1)[:, :, key_col]
        v_r = lax.dynamic_slice_in_dim(vg, rs, kh, axis=1)[:, :, key_col]
        row_off = rs + jnp.arange(kh) - r + NA_KH - 1
        bias = jnp.take(rpb_col, row_off, axis=1).transpose(0, 2, 3, 1, 4)
        sc = jnp.einsum("bnqhd,bknjhd->bhnqkj", q_r, k_r).astype(jnp.float32) * scale
        sc = jnp.where(mask, sc + bias[None], NEG)
        p = jax.nn.softmax(sc.reshape(bsz, h, N_COL_BLOCKS, Q_BLOCK_W, kh * K_BLOCK_W), axis=-1)
        p = p.reshape(sc.shape).astype(v.dtype)
        o = jnp.einsum("bhnqkj,bknjhd->bnqhd", p, v_r)
        return o.reshape(bsz, GRID_W, h * dh)

    out = lax.map(one_row, jnp.arange(rows))
    return out.transpose(1, 0, 2, 3).reshape(bsz, s, h * dh)


def setup_inputs(seed: int = 0) -> dict:
    key = jax.random.key(seed)
    ks = jax.random.split(key, 20)
    n = jax.random.normal
    f32 = jnp.float32
    L = DEPTH
    return {
        "x": n(ks[0], (BATCH, SEQ, D_MODEL), f32),
        "norm1_g": 1.0 + 0.05 * n(ks[1], (L, D_MODEL), f32),
        "w_in": n(ks[2], (L, D_MODEL, D_IN), f32) * D_MODEL ** -0.5,
        "b_in": 0.01 * n(ks[3], (L, D_IN), f32),
        "rpb": 0.1 * n(ks[4], (L, NA_HEADS, 2 * NA_KH - 1, 2 * NA_KW - 1), f32),
        "w_na_out": n(ks[5], (L, D_ATTN, D_MODEL), f32) * D_ATTN ** -0.5,
        "conv_dw_w": n(ks[6], (L, CONV_K, CONV_CH), f32) * CONV_K ** -0.5,
        "conv_dw_b": 0.01 * n(ks[7], (L, CONV_CH), f32),
        "conv_ln_g": 1.0 + 0.05 * n(ks[8], (L, CONV_CH), f32),
        "conv_ln_b": 0.01 * n(ks[9], (L, CONV_CH), f32),
        "w_conv_out": n(ks[10], (L, CONV_CH, D_MODEL), f32) * CONV_CH ** -0.5,
        "w_out": n(ks[11], (L, D_MODEL, D_MODEL), f32) * D_MODEL ** -0.5,
        "norm2_g": 1.0 + 0.05 * n(ks[12], (L, D_MODEL), f32),
        "w_up": n(ks[13], (L, D_MODEL, 2 * D_FF), f32) * D_MODEL ** -0.5,
        "ffn_dw_w": n(ks[14], (L, FFN_CONV_K, 2 * D_FF), f32) * FFN_CONV_K ** -0.5,
        "ffn_dw_b": 0.01 * n(ks[15], (L, 2 * D_FF), f32),
        "w_down": n(ks[16], (L, D_FF, D_MODEL), f32) * D_FF ** -0.5,
        "norm_f_g": 1.0 + 0.05 * n(ks[17], (D_MODEL,), f32),
    }


def reference(x, norm1_g, w_in, b_in, rpb, w_na_out, conv_dw_w, conv_dw_b, conv_ln_g,
              conv_ln_b, w_conv_out, w_out, norm2_g, w_up, ffn_dw_w, ffn_dw_b, w_down,
              norm_f_g):
    bsz, s, _ = x.shape
    splits = np.cumsum([D_ATTN, D_ATTN, D_ATTN, CONV_CH, CONV_CH, D_MODEL]).tolist()
    for l in range(DEPTH):
        h = _rmsnorm(x, norm1_g[l])
        z = h @ w_in[l] + b_in[l]
        q, k, v, c_a, c_b, g_a, g_b = jnp.split(z, splits, axis=-1)
        qh = q.reshape(bsz, s, NA_HEADS, NA_HEAD_DIM)
        kh = k.reshape(bsz, s, NA_HEADS, NA_HEAD_DIM)
        vh = v.reshape(bsz, s, NA_HEADS, NA_HEAD_DIM)
        br_a = _neighbourhood_attention(qh, kh, vh, rpb[l]) @ w_na_out[l]
        u = c_a * jax.nn.sigmoid(c_b)
        u = _dwconv_centred(u, conv_dw_w[l], conv_dw_b[l])
        u = jax.nn.silu(_layernorm(u, conv_ln_g[l], conv_ln_b[l]))
        br_b = u @ w_conv_out[l]
        merged = jax.nn.sigmoid(g_a) * br_a + jax.nn.sigmoid(g_b) * br_b
        x = x + merged @ w_out[l]
        h2 = _rmsnorm(x, norm2_g[l])
        up = _dwconv_centred(h2 @ w_up[l], ffn_dw_w[l], ffn_dw_b[l])
        gate, val = jnp.split(up, 2, axis=-1)
        x = x + (jax.nn.gelu(gate, approximate=False) * val) @ w_down[l]
    return _rmsnorm(x, norm_f_g)
```

```python
import numpy as np
from contextlib import ExitStack
import concourse.bass as bass
import concourse.mybir as mybir
from concourse.bass_utils import run_bass_kernel_spmd

F32 = mybir.dt.float32
BF16 = mybir.dt.bfloat16
AF = mybir.ActivationFunctionType
ALU = mybir.AluOpType

NT = 37
TE = NT * 128
Q0 = 256
NQT = 33
NQ = NQT * 128
OWN0 = 320
NOWN = 4096
EPS = 1e-6
NEGV = -30000.0
_STOP = None
_SCRATCH_KIND = "Internal"
_BPAIRS = None
_BDBG = 99
_BHEADS = list(range(8))

C_BIN = 0
C_CW = C_BIN + 36
C_CB = C_CW + 124
C_LG = C_CB + 4
C_LB = C_LG + 4
C_FW = C_LB + 4
C_FB = C_FW + 132
C_VT = C_FB + 44
NPP = C_VT + NT

EDGE_PAIRS = {0: list(range(0, 7)), 1: list(range(1, 7)), 2: list(range(2, 7)),
              30: list(range(30, 35)), 31: list(range(30, 36)), 32: list(range(30, 37))}
N_EDGE = sum(len(v) for v in EDGE_PAIRS.values())


class Dep:
    __slots__ = ("w", "r")

    def __init__(self):
        self.w = {}
        self.r = {}


class DSem:
    def __init__(self, key, sem):
        self.key = key
        self.sem = sem
        self.count = 0


class Sched:
    def __init__(self, nc, stack):
        self.nc = nc
        self.stack = stack
        self.E = {"pe": nc.tensor, "act": nc.scalar, "dve": nc.vector, "pool": nc.gpsimd, "sp": nc.sync}
        self.sems = {}
        self.cnt = {}
        for n in self.E:
            self.sems[n] = stack.enter_context(nc.semaphore("s_" + n))
            self.cnt[n] = 0
        self.seen = {n: {} for n in self.E}
        self.dsems = []
        self.pend = []

    def dsem(self):
        key = "d%d" % len(self.dsems)
        d = DSem(key, self.stack.enter_context(self.nc.semaphore("s_" + key)))
        self.sems[key] = d.sem
        self.dsems.append(d)
        return d

    def _wait(self, eng, key, val):
        if eng == "pe" and key == "pe":
            return
        if self.seen[eng].get(key, 0) >= val:
            return
        self.E[eng].wait_ge(self.sems[key], val)
        self.seen[eng][key] = val

    def _need(self, reads, writes):
        need = {}
        for d in reads:
            for k, v in d.w.items():
                if need.get(k, 0) < v:
                    need[k] = v
        for d in writes:
            for src in (d.w, d.r):
                for k, v in src.items():
                    if need.get(k, 0) < v:
                        need[k] = v
        return need

    @staticmethod
    def _mark(reads, writes, key, val):
        for d in reads:
            if d.r.get(key, 0) < val:
                d.r[key] = val
        for d in writes:
            if d.w.get(key, 0) < val:
                d.w[key] = val

    def run(self, eng, fn, reads=(), writes=(), inc=True):
        for k, v in self._need(reads, writes).items():
            self._wait(eng, k, v)
        ins = fn(self.E[eng])
        if eng == "pe" and not inc:
            self.pend.append((reads, writes))
            return
        self.cnt[eng] += 1
        ins.then_inc(self.sems[eng], 1)
        self._mark(reads, writes, eng, self.cnt[eng])
        if eng == "pe":
            for r, w in self.pend:
                self._mark(r, w, eng, self.cnt[eng])
            self.pend = []

    def dma(self, q, ds, out, in_, reads=(), writes=()):
        if q == "pool":
            ds = self.dsem()
        for k, v in self._need(reads, writes).items():
            self._wait(q, k, v)
        ins = self.E[q].dma_start(out=out, in_=in_)
        ds.count += 16
        ins.then_inc(ds.sem, 16)
        self._mark(reads, writes, ds.key, ds.count)

    def barrier(self):
        for eng in self.E:
            for n in self.E:
                if n != eng and self.cnt[n] > 0:
                    self._wait(eng, n, self.cnt[n])
            for d in self.dsems:
                if d.count > 0:
                    self._wait(eng, d.key, d.count)


def pipeline(nblocks, stages):
    ns = len(stages)
    for t in range(nblocks + ns - 1):
        for st_i in reversed(range(ns)):
            j = t - st_i
            if 0 <= j < nblocks:
                stages[st_i](j)


def build_nc():
    nc = bass.Bass("TRN2", target_bir_lowering=False)
    dt = lambda name, shape, dtype, kind: nc.dram_tensor(name, shape, dtype, kind=kind).ap()
    x_ext = dt("x_ext", [TE, 1024], F32, "ExternalInput")
    pp_d = dt("pp", [128, NPP], F32, "ExternalInput")
    g1b_d = dt("g1b", [128, 1024], F32, "ExternalInput")
    g2b_d = dt("g2b", [128, 1024], F32, "ExternalInput")
    gfb_d = dt("gfb", [128, 1024], F32, "ExternalInput")
    bvb_d = dt("bvb", [128, 512], F32, "ExternalInput")
    vmf_d = dt("vmf", [128, TE], F32, "ExternalInput")
    tstd_d = dt("tstd", [5, 128, 1024], F32, "ExternalInput")
    tedge_d = dt("tedge", [N_EDGE, 128, 1024], F32, "ExternalInput")
    idn_d = dt("idn", [128, 128], F32, "ExternalInput")
    w_in_d = dt("w_in", [1024, 4608], F32, "ExternalInput")
    w_na_d = dt("w_na", [512, 1024], F32, "ExternalInput")
    w_co_d = dt("w_co", [512, 1024], F32, "ExternalInput")
    w_out_d = dt("w_out", [1024, 1024], F32, "ExternalInput")
    w_up_d = dt("w_up", [1024, 5632], F32, "ExternalInput")
    w_dn_d = dt("w_dn", [2816, 1024], F32, "ExternalInput")
    out_d = dt("out", [NOWN, 1024], F32, "ExternalOutput")
    QT_d = dt("QT_s", [512, TE], BF16, _SCRATCH_KIND)
    KT_d = dt("KT_s", [512, TE], BF16, _SCRATCH_KIND)
    V_d = dt("V_s", [TE, 512], BF16, _SCRATCH_KIND)
    U_d = dt("U_s", [512, TE], BF16, _SCRATCH_KIND)
    SA_d = dt("SA_s", [1024, TE], BF16, _SCRATCH_KIND)
    SB_d = dt("SB_s", [1024, TE], BF16, _SCRATCH_KIND)
    BRA_d = dt("BRA_s", [1024, NQ], F32, _SCRATCH_KIND)
    X1_d = dt("X1_s", [NQ, 1024], F32, _SCRATCH_KIND)
    H2T_d = dt("H2T_s", [1024, NQ], BF16, _SCRATCH_KIND)

    fm = lambda ap: ap.rearrange("(c p) t -> p c t", p=128)

    with ExitStack() as top:
        S = Sched(nc, top)
        sb = lambda st, n, s, d: st.enter_context(nc.sbuf_tensor(n, s, d))
        ps = lambda st, n, s, d: st.enter_context(nc.psum_tensor(n, s, d))

        pp = sb(top, "pp_t", [128, NPP], F32)
        idb = sb(top, "idb", [128, 128], BF16)
        ones = sb(top, "ones", [128, 128], BF16)
        onesf = sb(top, "onesf", [128, 128], F32)
        junk = sb(top, "junk", [128, 1024], BF16)
        stat = sb(top, "stat", [128, 8], F32)
        D_const = Dep()
        D_junk = Dep()
        ds_c = S.dsem()
        S.dma("sp", ds_c, pp[:], pp_d, writes=[D_const])
        S.dma("pool", ds_c, idb[:], idn_d, writes=[D_const])
        S.run("dve", lambda e: e.memset(ones[:], 1.0), writes=[D_const])
        S.run("dve", lambda e: e.memset(onesf[:], 1.0 / 512.0), writes=[D_const])

        bq8 = sb(top, "bq8", [128, 4], F32)
        S.run("dve", lambda e: e.tensor_scalar(out=bq8[:], in0=pp[:, C_BIN:C_BIN + 4], scalar1=0.125, scalar2=None,
                                               op0=ALU.mult), reads=[D_const], writes=[D_const])

        def rstd_chain(ssum, outcol, D_ss, D_rs, vt_col=None):
            S.run("dve", lambda e: e.tensor_scalar(out=outcol, in0=ssum, scalar1=1.0 / 1024, scalar2=EPS,
                                                   op0=ALU.mult, op1=ALU.add), reads=[D_ss], writes=[D_rs])
            S.run("act", lambda e: e.activation(out=outcol, in_=outcol, func=AF.Sqrt), reads=[D_rs], writes=[D_rs])
            S.run("dve", lambda e: e.reciprocal(out=outcol, in_=outcol), reads=[D_rs], writes=[D_rs])
            if vt_col is not None:
                S.run("dve", lambda e: e.tensor_scalar(out=outcol, in0=outcol, scalar1=vt_col, scalar2=None,
                                                       op0=ALU.mult), reads=[D_rs, D_const], writes=[D_rs])

        stAB = top.enter_context(ExitStack())
        Wna = sb(stAB, "Wna", [128, 4, 1024], BF16)
        tstd = sb(stAB, "tstd_t", [128, 5, 1024], BF16)
        D_lB = Dep()

        with ExitStack() as st:
            Win = sb(st, "Win", [128, 8, 4608], BF16)
            g1b = sb(st, "g1b_t", [128, 1024], F32)
            bvb = sb(st, "bvb_t", [128, 512], F32)
            D_w = Dep()
            D_wg = [Dep() for _ in range(9)]
            ds_w = S.dsem()
            S.dma("sp", ds_w, g1b[:], g1b_d, writes=[D_w])
            S.dma("sp", ds_w, bvb[:], bvb_d, writes=[D_w])
            w_in_v = w_in_d.rearrange("(k p) n -> p k n", p=128)
            for g in range(9):
                S.dma("pool", ds_w, Win[:, :, g * 512:(g + 1) * 512], w_in_v[:, :, g * 512:(g + 1) * 512], writes=[D_wg[g]])
            S.dma("pool", ds_w, Wna[:], w_na_d.rearrange("(c p) n -> p c n", p=128), writes=[D_lB])
            S.dma("pool", ds_w, tstd[:], tstd_d.rearrange("c p n -> p c n"), writes=[D_lB])

            xt = [sb(st, "xt%d" % i, [128, 1024], F32) for i in range(2)]
            D_xt = [Dep(), Dep()]
            ds_xt = [S.dsem(), S.dsem()]
            hn = [sb(st, "hn%d" % i, [128, 1024], BF16) for i in range(2)]
            D_hn = [Dep(), Dep()]
            hT = [sb(st, "hT%d" % i, [128, 8, 512], BF16) for i in range(2)]
            D_hT = [Dep(), Dep()]
            ptr = [ps(st, "ptrA%d" % i, [128, 8, 128], BF16) for i in range(2)]
            D_ptr = [Dep(), Dep()]
            pacc = [ps(st, "pacc%d" % i, [128, 512], F32) for i in range(4)]
            D_pacc = [Dep() for _ in range(4)]
            pv = [ps(st, "pvA%d" % i, [128, 512], F32) for i in range(2)]
            D_pv = [Dep(), Dep()]
            Qb = [sb(st, "Qb%d" % i, [128, 4, 512], BF16) for i in range(2)]
            Kb = [sb(st, "Kb%d" % i, [128, 4, 512], BF16) for i in range(2)]
            Ub = [sb(st, "Ub%d" % i, [128, 4, 512], BF16) for i in range(2)]
            SAb = [sb(st, "SAb%d" % i, [128, 8, 512], BF16) for i in range(2)]
            SBb = [sb(st, "SBb%d" % i, [128, 8, 512], BF16) for i in range(2)]
            Vb = [sb(st, "Vb%d" % i, [128, 4, 512], BF16) for i in range(2)]
            D_ob = {n: [Dep(), Dep()] for n in ("Q", "K", "U", "SA", "SB", "V")}
            ds_ob = {n: [S.dsem(), S.dsem()] for n in ("Q", "K", "U", "SA", "SB", "V")}
            tma = [sb(st, "tma%d" % i, [128, 512], F32) for i in range(2)]
            tmb = [sb(st, "tmb%d" % i, [128, 512], F32) for i in range(2)]
            D_tma = [Dep(), Dep()]
            D_tmb = [Dep(), Dep()]
            vms = [sb(st, "vms%d" % i, [128, 512], F32) for i in range(2)]
            D_vms = [Dep(), Dep()]
            ds_vms = [S.dsem(), S.dsem()]
            D_ss = Dep()
            D_rs = Dep()
            D_scr = Dep()

            nblk = (NT + 3) // 4
            acc_state = [0]

            def blk(b):
                tiles = list(range(4 * b, min(4 * b + 4, NT)))
                return tiles, 128 * len(tiles), 512 * b, b % 2

            def a_s0(b, ti, part):
                if b >= nblk:
                    return
                tiles, n, t0, bs = blk(b)
                if ti == 0 and part == 0:
                    S.dma("sp", ds_vms[bs], vms[bs][:, 0:n], vmf_d[:, t0:t0 + n], writes=[D_vms[bs]])
                if ti >= len(tiles):
                    return
                t = tiles[ti]
                s_ = t % 2
                if part == 0:
                    S.dma("sp", ds_xt[s_], xt[s_][:], x_ext[t * 128:(t + 1) * 128, :], writes=[D_xt[s_]])
                    S.run("act", lambda e: e.activation(out=junk[:], in_=xt[s_][:], func=AF.Square,
                                                        accum_out=stat[:, 0:1]),
                          reads=[D_xt[s_]], writes=[D_junk, D_ss])
                    rstd_chain(stat[:, 0:1], stat[:, 1:2], D_ss, D_rs)
                    S.run("dve", lambda e: e.scalar_tensor_tensor(out=hn[s_][:], in0=xt[s_][:], scalar=stat[:, 1:2],
                                                                  in1=g1b[:], op0=ALU.mult, op1=ALU.mult),
                          reads=[D_xt[s_], D_rs, D_w], writes=[D_hn[s_]])
                else:
                    for k in range(8):
                        S.run("pe", lambda e: e.transpose(ptr[s_][:, k, :], hn[s_][:, k * 128:(k + 1) * 128], idb[:]),
                              reads=[D_hn[s_], D_const], writes=[D_ptr[s_]], inc=(k == 7))
                    S.run("dve", lambda e: e.tensor_copy(out=hT[bs][:, :, ti * 128:(ti + 1) * 128], in_=ptr[s_][:]),
                          reads=[D_ptr[s_]], writes=[D_hT[bs]])

            def a_s1(b):
                tiles, n, t0, bs = blk(b)

                def proj(fc, evac):
                    a = acc_state[0] % 4
                    acc_state[0] += 1
                    for k in range(8):
                        S.run("pe", lambda e: e.matmul(pacc[a][:, 0:n], lhsT=Win[:, k, fc * 128:(fc + 1) * 128],
                                                       rhs=hT[bs][:, k, 0:n], start=(k == 0), stop=(k == 7)),
                              reads=[D_wg[fc // 4], D_hT[bs]], writes=[D_pacc[a]], inc=(k == 7))
                    evac(pacc[a], D_pacc[a])

                def evac_to(buf, Db, c, func, fc):
                    def f(pa, Dpa):
                        S.run("act", lambda e: e.activation(out=buf[:, c, 0:n], in_=pa[:, 0:n], func=func,
                                                            bias=pp[:, C_BIN + fc:C_BIN + fc + 1]),
                              reads=[Dpa, D_const], writes=[Db])
                    return f

                for c in range(4):
                    def ev_q(pa, Dpa, c=c):
                        S.run("act", lambda e: e.activation(out=Qb[bs][:, c, 0:n], in_=pa[:, 0:n], func=AF.Identity,
                                                            bias=bq8[:, c:c + 1], scale=0.125),
                              reads=[Dpa, D_const], writes=[D_ob["Q"][bs]])
                    proj(c, ev_q)
                a_s0(b + 1, 0, 0)
                for c in range(4):
                    proj(4 + c, evac_to(Kb[bs], D_ob["K"][bs], c, AF.Identity, 4 + c))
                a_s0(b + 1, 0, 1)
                a_s0(b + 1, 1, 0)
                for ti, t in enumerate(tiles):
                    v = ti % 2
                    for k in range(8):
                        S.run("pe", lambda e: e.matmul(pv[v][:, 0:512], lhsT=hT[bs][:, k, ti * 128:(ti + 1) * 128],
                                                       rhs=Win[:, k, 1024:1536], start=(k == 0), stop=(k == 7)),
                              reads=[D_wg[2], D_hT[bs]], writes=[D_pv[v]], inc=(k == 7))
                    S.run("dve", lambda e: e.tensor_tensor(out=Vb[bs][:, ti, :], in0=pv[v][:], in1=bvb[:], op=ALU.add),
                          reads=[D_pv[v], D_w], writes=[D_ob["V"][bs]])
                for c in range(4):
                    g = c % 2
                    fa, fb = 12 + c, 16 + c

                    def ev_a(pa, Dpa):
                        S.run("act", lambda e: e.activation(out=tma[g][:, 0:n], in_=pa[:, 0:n], func=AF.Identity,
                                                            bias=pp[:, C_BIN + fa:C_BIN + fa + 1]),
                              reads=[Dpa, D_const], writes=[D_tma[g]])

                    def ev_b(pa, Dpa):
                        S.run("act", lambda e: e.activation(out=tmb[g][:, 0:n], in_=pa[:, 0:n], func=AF.Sigmoid,
                                                            bias=pp[:, C_BIN + fb:C_BIN + fb + 1]),
                              reads=[Dpa, D_const], writes=[D_tmb[g]])
                    proj(fa, ev_a)
                    proj(fb, ev_b)
                    S.run("dve", lambda e: e.tensor_tensor(out=tma[g][:, 0:n], in0=tma[g][:, 0:n], in1=tmb[g][:, 0:n],
                                                           op=ALU.mult),
                          reads=[D_tmb[g]], writes=[D_tma[g]])
                    S.run("pool", lambda e: e.tensor_tensor(out=Ub[bs][:, c, 0:n], in0=tma[g][:, 0:n],
                                                            in1=vms[bs][:, 0:n], op=ALU.mult),
                          reads=[D_tma[g], D_vms[bs]], writes=[D_ob["U"][bs]])
                a_s0(b + 1, 1, 1)
                a_s0(b + 1, 2, 0)
                for c in range(8):
                    proj(20 + c, evac_to(SAb[bs], D_ob["SA"][bs], c, AF.Sigmoid, 20 + c))
                a_s0(b + 1, 2, 1)
                a_s0(b + 1, 3, 0)
                for c in range(8):
                    proj(28 + c, evac_to(SBb[bs], D_ob["SB"][bs], c, AF.Sigmoid, 28 + c))
                a_s0(b + 1, 3, 1)
                q = "sp"
                S.dma(q, ds_ob["Q"][bs], fm(QT_d)[:, :, t0:t0 + n], Qb[bs][:, :, 0:n], reads=[D_ob["Q"][bs]], writes=[D_scr])
                S.dma(q, ds_ob["K"][bs], fm(KT_d)[:, :, t0:t0 + n], Kb[bs][:, :, 0:n], reads=[D_ob["K"][bs]], writes=[D_scr])
                S.dma(q, ds_ob["V"][bs],
                      V_d[t0:t0 + n, :].rearrange("(t p) n -> p t n", p=128), Vb[bs][:, 0:len(tiles), :],
                      reads=[D_ob["V"][bs]], writes=[D_scr])
                S.dma(q, ds_ob["U"][bs], fm(U_d)[:, :, t0:t0 + n], Ub[bs][:, :, 0:n], reads=[D_ob["U"][bs]], writes=[D_scr])
                S.dma(q, ds_ob["SA"][bs], fm(SA_d)[:, :, t0:t0 + n], SAb[bs][:, :, 0:n], reads=[D_ob["SA"][bs]], writes=[D_scr])
                S.dma(q, ds_ob["SB"][bs], fm(SB_d)[:, :, t0:t0 + n], SBb[bs][:, :, 0:n], reads=[D_ob["SB"][bs]], writes=[D_scr])

            for ti in range(4):
                a_s0(0, ti, 0)
                a_s0(0, ti, 1)
            for b in range(nblk):
                a_s1(b)
            S.barrier()
        if _STOP == "A":
            return nc

        with ExitStack() as st:
            KTs = sb(st, "KTs", [128, 4, TE], BF16)
            Vs = sb(st, "Vs", [128, NT, 512], BF16)
            D_l = D_lB
            ds_l = S.dsem()
            v_v = V_d.rearrange("(t p) n -> p t n", p=128)
            NKG = (NT + 7) // 8
            D_kv = [Dep() for _ in range(NKG)]
            ds_kv = [S.dsem() for _ in range(NKG)]
            def load_kv(g):
                t, te = 8 * g, min(8 * g + 8, NT)
                S.dma("sp", ds_kv[g], KTs[:, :, t * 128:te * 128], fm(KT_d)[:, :, t * 128:te * 128], writes=[D_kv[g]])
                S.dma("sp", ds_kv[g], Vs[:, t:te, :], v_v[:, t:te, :], writes=[D_kv[g]])

            load_kv(0)

            NEB = 4
            eb = [sb(st, "eb%d" % i, [128, 1024], BF16) for i in range(NEB)]
            eb32 = [sb(st, "eb32_%d" % i, [128, 1024], F32) for i in range(NEB)]
            D_eb32 = [Dep() for _ in range(NEB)]
            D_eb = [Dep() for _ in range(NEB)]
            ds_eb = [S.dsem() for _ in range(NEB)]
            Spl = [ps(st, "Sp%d" % i, [128, 1024], F32) for i in range(2)]
            D_Spl = [Dep(), Dep()]
            Op = ps(st, "Op", [128, 1024], F32)
            SUp = ps(st, "SUp", [128, 1024], F32)
            NAp = SUp[:].rearrange("p (a b) -> p a b", a=8)
            D_Op, D_SUp = Dep(), Dep()
            D_NAp = D_SUp
            rcp = sb(st, "rcp", [128, 1024], F32)
            D_rcp = Dep()
            nab = [sb(st, "nab%d" % i, [128, 8, 128], F32) for i in range(2)]
            D_nab = [Dep(), Dep()]
            ds_nab = [S.dsem(), S.dsem()]
            D_bra = Dep()
            Qm = [sb(st, "Qm%d" % i, [128, 4, 2, 128], BF16) for i in range(2)]
            D_Qm = [Dep(), Dep()]
            ds_Qm = [S.dsem(), S.dsem()]
            SAt = [sb(st, "SAt%d" % i, [128, 8, 128], BF16) for i in range(3)]
            D_SAt = [Dep(), Dep(), Dep()]
            ds_SAt = [S.dsem(), S.dsem(), S.dsem()]
            for i in range(2):
                S.run("dve", lambda e: e.memset(Qm[i][:], 0.0), writes=[D_Qm[i]])
            OTz = sb(st, "OTz", [128, 4, 128], BF16)
            D_OTz = Dep()
            Op4 = Op[:].rearrange("p (a b c) -> p a b c", a=4, b=2)
            rcp4 = rcp[:].rearrange("p (a b c) -> p a b c", a=4, b=2)
            NPT = 7
            PT = [[sb(st, "PT%d_%d" % (g, i), [128, 1024], BF16) for i in range(NPT)] for g in range(2)]
            D_PT = [[Dep() for _ in range(NPT)] for g in range(2)]

            def load_edge(idx):
                sl = idx % NEB
                S.dma("sp", ds_eb[sl], eb32[sl][:], tedge_d[idx], writes=[D_eb32[sl]])
                if idx % 2 == 0:
                    S.run("act", lambda e: e.copy(out=eb[sl][:], in_=eb32[sl][:]), reads=[D_eb32[sl]], writes=[D_eb[sl]])
                else:
                    S.run("dve", lambda e: e.tensor_copy(out=eb[sl][:], in_=eb32[sl][:]), reads=[D_eb32[sl]],
                          writes=[D_eb[sl]])

            def load_q(i):
                qs = i % 2
                qt = i + 2
                S.dma("sp", ds_Qm[qs], Qm[qs][0:64, :, 0, :], fm(QT_d)[0:64, :, qt * 128:(qt + 1) * 128], writes=[D_Qm[qs]])
                S.dma("sp", ds_Qm[qs], Qm[qs][64:128, :, 1, :], fm(QT_d)[64:128, :, qt * 128:(qt + 1) * 128],
                      writes=[D_Qm[qs]])
                S.dma("sp", ds_SAt[i % 3], SAt[i % 3][:], fm(SA_d)[:, :, qt * 128:(qt + 1) * 128], writes=[D_SAt[i % 3]])

            pair_kts = [EDGE_PAIRS.get(i, list(range(i, i + 5))) for i in range(NQT)]
            edge_base = {}
            eacc = 0
            for i in range(NQT):
                if i in EDGE_PAIRS:
                    edge_base[i] = eacc
                    eacc += len(pair_kts[i])
            for idx in range(NEB - 1):
                load_edge(idx)
            load_q(0)
            kv_issue = {1: 1, 6: 2, 14: 3, 22: 4}
            cstate = [0]

            def s_chunk(i, j):
                qt = i + 2
                qs = i % 2
                g = i % 2
                kt = pair_kts[i][j]
                cs = cstate[0] % 2
                cstate[0] += 1
                Sp, D_Sp = Spl[cs], D_Spl[cs]
                if i in EDGE_PAIRS:
                    ei = edge_base[i] + j
                    nxt = ei + NEB - 1
                    if nxt < N_EDGE:
                        load_edge(nxt)
                    tb = eb[ei % NEB][:]
                    D_tb = D_eb[ei % NEB]
                else:
                    tb = tstd[:, j, :]
                    D_tb = D_l
                for hf in range(2):
                    S.run("pe", lambda e: e.matmul(Sp[:, hf * 512:(hf + 1) * 512], lhsT=idb[:],
                                                   rhs=tb[:, hf * 512:(hf + 1) * 512], start=True, stop=False),
                          reads=[D_const, D_tb], writes=[D_Sp], inc=False)
                    for h in range(4 * hf, 4 * hf + 4):
                        S.run("pe", lambda e: e.matmul(Sp[:, h * 128:(h + 1) * 128],
                                                       lhsT=KTs[:, h // 2, kt * 128:(kt + 1) * 128],
                                                       rhs=Qm[qs][:, h // 2, h % 2, :], start=False, stop=(h % 4 == 3)),
                              reads=[D_kv[kt // 8], D_Qm[qs]], writes=[D_Sp], inc=(h == 7))
                S.run("act", lambda e: e.activation(out=PT[g][j][:], in_=Sp[:], func=AF.Exp),
                      reads=[D_Sp], writes=[D_PT[g][j]])

            def pv_heads(i, heads):
                g = i % 2
                kts = pair_kts[i]
                nk = len(kts)
                for h in heads:
                    for j, kt in enumerate(kts):
                        S.run("pe", lambda e: e.matmul(Op[:, h * 128:(h + 1) * 128],
                                                       lhsT=Vs[:, kt, (h // 2) * 128:(h // 2 + 1) * 128],
                                                       rhs=PT[g][j][:, h * 128:(h + 1) * 128],
                                                       start=(j == 0), stop=(j == nk - 1)),
                              reads=[D_kv[kt // 8], D_PT[g][j]], writes=[D_Op], inc=(h == 7 and j == nk - 1))

            def tail(i, part):
                qs = i % 2
                g = i % 2
                nk = len(pair_kts[i])
                if part == 0:
                    tail_a(i, g, nk)
                else:
                    tail_b(i, qs)

            def tail_a(i, g, nk):
                for hf in range(2):
                    for j in range(nk):
                        S.run("pe", lambda e: e.matmul(SUp[:, hf * 512:(hf + 1) * 512], lhsT=ones[:, 0:128],
                                                       rhs=PT[g][j][:, hf * 512:(hf + 1) * 512],
                                                       start=(j == 0), stop=(j == nk - 1)),
                              reads=[D_const, D_PT[g][j]], writes=[D_SUp], inc=(hf == 1 and j == nk - 1))
                ns = i % 2
                S.run("act", lambda e: e.activation(out=rcp[:], in_=SUp[:], func=AF.Ln), reads=[D_SUp], writes=[D_rcp])
                S.run("act", lambda e: e.activation(out=rcp[:], in_=rcp[:], func=AF.Exp, scale=-1.0),
                      reads=[], writes=[D_rcp])
                S.run("dve", lambda e: e.tensor_tensor(out=OTz[0:64, :, :], in0=Op4[0:64, :, 0, :],
                                                       in1=rcp4[0:64, :, 0, :], op=ALU.mult),
                      reads=[D_Op, D_rcp], writes=[D_OTz])
                S.run("dve", lambda e: e.tensor_tensor(out=OTz[64:128, :, :], in0=Op4[64:128, :, 1, :],
                                                       in1=rcp4[64:128, :, 1, :], op=ALU.mult),
                      reads=[D_Op, D_rcp], writes=[D_OTz])

            def tail_b(i, qs):
                ns = i % 2
                for dc in range(8):
                    for c in range(4):
                        S.run("pe", lambda e: e.matmul(NAp[:, dc, :], lhsT=Wna[:, c, dc * 128:(dc + 1) * 128],
                                                       rhs=OTz[:, c, :], start=(c == 0), stop=(c == 3)),
                              reads=[D_l, D_OTz], writes=[D_NAp], inc=(dc == 7 and c == 3))
                S.run("dve", lambda e: e.tensor_tensor(out=nab[ns][:], in0=NAp, in1=SAt[i % 3][:], op=ALU.mult),
                      reads=[D_NAp, D_SAt[i % 3]], writes=[D_nab[ns]])
                S.dma("sp", ds_nab[ns], fm(BRA_d)[:, :, i * 128:(i + 1) * 128], nab[ns][:],
                      reads=[D_nab[ns]], writes=[D_bra])

            for i in range(NQT + 1):
                if i in kv_issue and kv_issue[i] < NKG:
                    load_kv(kv_issue[i])
                if i + 1 < NQT:
                    load_q(i + 1)
                nk = len(pair_kts[i]) if i < NQT else 0
                hsplit = [[0, 1], [2, 3], [4, 5], [6, 7]]
                for j in range(max(nk, 5)):
                    if j < nk:
                        s_chunk(i, j)
                    if i >= 1 and j < 4:
                        pv_heads(i - 1, hsplit[j])
                    if i >= 1 and j == 3:
                        tail(i - 1, 0)
                if i >= 1:
                    tail(i - 1, 1)
            S.barrier()
        stAB.close()
        if _STOP == "B":
            return nc

        stCD = top.enter_context(ExitStack())
        WupA = sb(stCD, "WupA", [128, 8, 3072], BF16)
        D_wu = [Dep() for _ in range(11)]
        w_up_v = w_up_d.rearrange("(k p) n -> p k n", p=128)

        def load_wup(g, W, gl, ds_):
            S.dma("pool", ds_, W[:, :, gl * 512:gl * 512 + 256], w_up_v[:, :, (2 * g) * 128:(2 * g + 2) * 128],
                  writes=[D_wu[g]])
            S.dma("pool", ds_, W[:, :, gl * 512 + 256:gl * 512 + 512],
                  w_up_v[:, :, (22 + 2 * g) * 128:(22 + 2 * g + 2) * 128], writes=[D_wu[g]])

        with ExitStack() as st:
            DG = sb(st, "DG", [128, 4, 31, 128], BF16)
            D_DGc = [Dep() for _ in range(4)]
            Wco = sb(st, "Wco", [128, 4, 1024], BF16)
            Wout = sb(st, "Wout", [128, 8, 1024], BF16)
            g2b = sb(st, "g2b_t", [128, 1024], F32)
            D_l = Dep()
            ds_l = S.dsem()
            S.dma("pool", ds_l, Wco[:], w_co_d.rearrange("(c p) n -> p c n", p=128), writes=[D_l])
            S.dma("pool", ds_l, Wout[:], w_out_d.rearrange("(c p) n -> p c n", p=128), writes=[D_l])
            S.dma("sp", ds_l, g2b[:], g2b_d, writes=[D_l])
            def build_dg(c):
                for k in range(31):
                    col = C_CW + c * 31 + k
                    if k % 2 == 0:
                        S.run("dve", lambda e: e.tensor_scalar(out=DG[:, c, k, :], in0=idb[:], scalar1=pp[:, col:col + 1],
                                                               scalar2=None, op0=ALU.mult),
                              reads=[D_const], writes=[D_DGc[c]])
                    else:
                        S.run("act", lambda e: e.activation(out=DG[:, c, k, :], in_=idb[:], func=AF.Copy,
                                                            scale=pp[:, col:col + 1]),
                              reads=[D_const], writes=[D_DGc[c]])

            for g in range(6):
                load_wup(g, WupA, g, ds_l)

            BN = 256
            Us = [sb(st, "Us%d" % i, [128, 4, BN + 30], BF16) for i in range(2)]
            D_Us = [Dep(), Dep()]
            ds_Us = [S.dsem(), S.dsem()]
            MAs = [sb(st, "MAs%d" % i, [128, 8, BN], F32) for i in range(2)]
            SBs = [sb(st, "SBs%d" % i, [128, 8, BN], BF16) for i in range(2)]
            D_in = [Dep(), Dep()]
            ds_in = [S.dsem(), S.dsem()]
            cv = [sb(st, "cv%d" % i, [128, 4, BN], F32) for i in range(2)]
            D_cv = [Dep(), Dep()]
            cvs = sb(st, "cvs", [128, 4, BN], F32)
            D_cvs = Dep()
            mean_s = sb(st, "mean_s", [128, BN], F32)
            rstd_s = sb(st, "rstd_s", [128, BN], F32)
            D_mean, D_rstd = Dep(), Dep()
            ua = [sb(st, "ua%d" % i, [128, 4, BN], BF16) for i in range(2)]
            D_ua = [Dep(), Dep()]
            t2 = [sb(st, "t2_%d" % i, [128, BN], F32) for i in range(2)]
            D_t2 = [Dep(), Dep()]
            mg = [sb(st, "mg%d" % i, [128, 8, BN], BF16) for i in range(2)]
            D_mg = [Dep(), Dep()]
            xt = [sb(st, "xtC%d" % i, [128, 1024], F32) for i in range(4)]
            D_xt = [Dep() for _ in range(4)]
            ds_xt = [S.dsem() for _ in range(4)]
            x1 = [sb(st, "x1C%d" % i, [128, 1024], F32) for i in range(2)]
            D_x1 = [Dep(), Dep()]
            ds_x1 = [S.dsem(), S.dsem()]
            h2b = [sb(st, "h2b%d" % i, [128, 2, 1024], BF16) for i in range(2)]
            D_h2b = [Dep(), Dep()]
            h2T = [sb(st, "h2T%d" % i, [128, 8, 128], BF16) for i in range(2)]
            D_h2T = [Dep(), Dep()]
            ds_h2T = [S.dsem(), S.dsem()]
            CVp = [ps(st, "CVp%d" % i, [128, 512], F32) for i in range(2)]
            D_CVp = [Dep(), Dep()]
            STp = ps(st, "STp", [128, 512], F32)
            D_STp = Dep()
            BRp = [ps(st, "BRp%d" % i, [128, 512], F32) for i in range(2)]
            D_BRp = [Dep(), Dep()]
            OPp = [ps(st, "OPp%d" % i, [128, 512], F32) for i in range(2)]
            D_OPp = [Dep(), Dep()]
            PTp = ps(st, "PTp", [128, 8, 128], BF16)
            D_PTp = Dep()
            D_ss, D_rs = Dep(), Dep()
            D_scr = Dep()

            nblk = (NQT + 1) // 2
            blocks = []
            for j in range(nblk):
                nt_ = min(2, NQT - 2 * j)
                blocks.append((Q0 + BN * j, 128 * nt_, nt_))

            def load_us(j):
                t0, n, _ = blocks[j]
                s_ = j % 2
                S.dma("sp", ds_Us[s_], Us[s_][:, :, 0:n + 30], fm(U_d)[:, :, t0 - 15:t0 + n + 15], writes=[D_Us[s_]])

            def c_s0(j):
                t0, n, nt_ = blocks[j]
                s_ = j % 2
                if j + 1 < nblk:
                    load_us(j + 1)
                for c in range(4):
                    p = c % 2
                    if j == 0:
                        build_dg(c)
                    for k in range(31):
                        S.run("pe", lambda e: e.matmul(CVp[p][:, 0:n], lhsT=DG[:, c, k, :], rhs=Us[s_][:, c, k:k + n],
                                                       start=(k == 0), stop=(k == 30)),
                              reads=[D_DGc[c], D_Us[s_]], writes=[D_CVp[p]], inc=(k == 30))
                    S.run("act", lambda e: e.activation(out=cv[s_][:, c, 0:n], in_=CVp[p][:, 0:n], func=AF.Identity,
                                                        bias=pp[:, C_CB + c:C_CB + c + 1]),
                          reads=[D_CVp[p], D_const], writes=[D_cv[s_]])

            def c_s1(j):
                t0, n, nt_ = blocks[j]
                s_ = j % 2
                S.dma("sp", ds_in[s_], MAs[s_][:, :, 0:n], fm(BRA_d)[:, :, t0 - Q0:t0 - Q0 + n], writes=[D_in[s_]])
                S.dma("sp", ds_in[s_], SBs[s_][:, :, 0:n], fm(SB_d)[:, :, t0:t0 + n], writes=[D_in[s_]])
                S.run("dve", lambda e: e.tensor_tensor(out=cvs[:, :, 0:n], in0=cv[s_][:, :, 0:n], in1=cv[s_][:, :, 0:n],
                                                       op=ALU.mult), reads=[D_cv[s_]], writes=[D_cvs])
                for c in range(4):
                    S.run("pe", lambda e: e.matmul(STp[:, 0:n], lhsT=onesf[:], rhs=cv[s_][:, c, 0:n],
                                                   start=(c == 0), stop=(c == 3)),
                          reads=[D_const, D_cv[s_]], writes=[D_STp], inc=(c == 3))
                for c in range(4):
                    S.run("pe", lambda e: e.matmul(STp[:, 256:256 + n], lhsT=onesf[:], rhs=cvs[:, c, 0:n],
                                                   start=(c == 0), stop=(c == 3)),
                          reads=[D_const, D_cvs], writes=[D_STp], inc=(c == 3))
                S.run("act", lambda e: e.copy(out=mean_s[:, 0:n], in_=STp[:, 0:n]), reads=[D_STp], writes=[D_mean])
                S.run("dve", lambda e: e.tensor_tensor(out=rstd_s[:, 0:n], in0=mean_s[:, 0:n], in1=mean_s[:, 0:n],
                                                       op=ALU.mult), reads=[D_mean], writes=[D_rstd])
                S.run("dve", lambda e: e.tensor_tensor(out=rstd_s[:, 0:n], in0=STp[:, 256:256 + n], in1=rstd_s[:, 0:n],
                                                       op=ALU.subtract), reads=[D_STp], writes=[D_rstd])
                S.run("dve", lambda e: e.tensor_scalar(out=rstd_s[:, 0:n], in0=rstd_s[:, 0:n], scalar1=0.0, scalar2=EPS,
                                                       op0=ALU.max, op1=ALU.add), reads=[], writes=[D_rstd])
                S.run("act", lambda e: e.activation(out=rstd_s[:, 0:n], in_=rstd_s[:, 0:n], func=AF.Sqrt),
                      reads=[], writes=[D_rstd])
                S.run("dve", lambda e: e.reciprocal(out=rstd_s[:, 0:n], in_=rstd_s[:, 0:n]), reads=[], writes=[D_rstd])
                for c in range(4):
                    S.run("dve", lambda e: e.tensor_tensor(out=cv[s_][:, c, 0:n], in0=cv[s_][:, c, 0:n],
                                                           in1=mean_s[:, 0:n], op=ALU.subtract),
                          reads=[D_mean, D_STp], writes=[D_cv[s_]])
                    S.run("dve", lambda e: e.tensor_tensor(out=cv[s_][:, c, 0:n], in0=cv[s_][:, c, 0:n],
                                                           in1=rstd_s[:, 0:n], op=ALU.mult),
                          reads=[D_rstd], writes=[D_cv[s_]])
                    S.run("act", lambda e: e.activation(out=ua[s_][:, c, 0:n], in_=cv[s_][:, c, 0:n], func=AF.Silu,
                                                        bias=pp[:, C_LB + c:C_LB + c + 1],
                                                        scale=pp[:, C_LG + c:C_LG + c + 1]),
                          reads=[D_cv[s_], D_const], writes=[D_ua[s_]])

            def c_s2(j):
                t0, n, nt_ = blocks[j]
                s_ = j % 2
                for tt in range(nt_):
                    xq = (j % 2) * 2 + tt
                    S.dma("sp", ds_xt[xq], xt[xq][:], x_ext[t0 + tt * 128:t0 + (tt + 1) * 128, :], writes=[D_xt[xq]])
                for dc in range(8):
                    p = dc % 2
                    for c in range(4):
                        S.run("pe", lambda e: e.matmul(BRp[p][:, 0:n], lhsT=Wco[:, c, dc * 128:(dc + 1) * 128],
                                                       rhs=ua[s_][:, c, 0:n], start=(c == 0), stop=(c == 3)),
                              reads=[D_l, D_ua[s_]], writes=[D_BRp[p]], inc=(c == 3))
                    S.run("dve", lambda e: e.tensor_tensor(out=t2[p][:, 0:n], in0=BRp[p][:, 0:n], in1=SBs[s_][:, dc, 0:n],
                                                           op=ALU.mult), reads=[D_BRp[p], D_in[s_]], writes=[D_t2[p]])
                    S.run("pool", lambda e: e.tensor_tensor(out=mg[s_][:, dc, 0:n], in0=MAs[s_][:, dc, 0:n],
                                                            in1=t2[p][:, 0:n], op=ALU.add),
                          reads=[D_in[s_], D_t2[p]], writes=[D_mg[s_]])

            def c_s3(j):
                t0, n, nt_ = blocks[j]
                s_ = j % 2
                for tt in range(nt_):
                    xs = tt
                    tile_i = (t0 // 128) + tt
                    xq = (j % 2) * 2 + tt
                    for hf in range(2):
                        for k in range(8):
                            S.run("pe", lambda e: e.matmul(OPp[hf][:, 0:512], lhsT=mg[s_][:, k, tt * 128:(tt + 1) * 128],
                                                           rhs=Wout[:, k, hf * 512:(hf + 1) * 512],
                                                           start=(k == 0), stop=(k == 7)),
                                  reads=[D_l, D_mg[s_]], writes=[D_OPp[hf]], inc=(k == 7))
                        S.run("dve", lambda e: e.tensor_tensor(out=x1[xs][:, hf * 512:(hf + 1) * 512], in0=OPp[hf][:, 0:512],
                                                               in1=xt[xq][:, hf * 512:(hf + 1) * 512], op=ALU.add),
                              reads=[D_OPp[hf], D_xt[xq]], writes=[D_x1[xs]])
                    S.dma("sp", ds_x1[xs], X1_d[t0 - Q0 + tt * 128:t0 - Q0 + (tt + 1) * 128, :], x1[xs][:],
                          reads=[D_x1[xs]], writes=[D_scr])
                    S.run("act", lambda e: e.activation(out=junk[:], in_=x1[xs][:], func=AF.Square,
                                                        accum_out=stat[:, 2:3]),
                          reads=[D_x1[xs]], writes=[D_junk, D_ss])
                    rstd_chain(stat[:, 2:3], stat[:, 3:4], D_ss, D_rs, vt_col=pp[:, C_VT + tile_i:C_VT + tile_i + 1])
                    S.run("dve", lambda e: e.scalar_tensor_tensor(out=h2b[s_][:, tt, :], in0=x1[xs][:], scalar=stat[:, 3:4],
                                                                  in1=g2b[:], op0=ALU.mult, op1=ALU.mult),
                          reads=[D_x1[xs], D_rs, D_l], writes=[D_h2b[s_]])

            def c_s4(j):
                t0, n, nt_ = blocks[j]
                s_ = j % 2
                for tt in range(nt_):
                    xs = tt
                    for k in range(8):
                        S.run("pe", lambda e: e.transpose(PTp[:, k, :], h2b[s_][:, tt, k * 128:(k + 1) * 128], idb[:]),
                              reads=[D_h2b[s_], D_const], writes=[D_PTp], inc=(k == 7))
                    S.run("act", lambda e: e.copy(out=h2T[xs][:], in_=PTp[:]), reads=[D_PTp], writes=[D_h2T[xs]])
                    c0_ = t0 - Q0 + tt * 128
                    S.dma("sp", ds_h2T[xs], fm(H2T_d)[:, :, c0_:c0_ + 128], h2T[xs][:], reads=[D_h2T[xs]], writes=[D_scr])

            load_us(0)
            pipeline(nblk, [c_s0, c_s1, c_s2, c_s3, c_s4])
            S.barrier()
        if _STOP == "C":
            return nc

        with ExitStack() as st:
            WupB = sb(st, "WupB", [128, 8, 2560], BF16)
            Wdn = sb(st, "Wdn", [128, 22, 1024], BF16)
            gfb = sb(st, "gfb_t", [128, 1024], F32)
            D_l = Dep()
            ds_l = S.dsem()
            for g in range(6, 11):
                load_wup(g, WupB, g - 6, ds_l)
            w_dn_v = w_dn_d.rearrange("(k p) n -> p k n", p=128)
            for k in range(0, 22, 6):
                ke = min(k + 6, 22)
                S.dma("pool", ds_l, Wdn[:, k:ke, :], w_dn_v[:, k:ke, :], writes=[D_l])
            S.dma("sp", ds_l, gfb[:], gfb_d, writes=[D_l])

            hs = [sb(st, "hsD%d" % i, [128, 8, 512], BF16) for i in range(2)]
            D_hs = [Dep(), Dep()]
            ds_hs = [S.dsem(), S.dsem()]
            gg = sb(st, "gg", [128, 22, 512], BF16)
            D_gg = Dep()
            ag = [sb(st, "ag%d" % i, [128, 512], F32) for i in range(2)]
            av = [sb(st, "av%d" % i, [128, 512], F32) for i in range(2)]
            D_ag = [Dep(), Dep()]
            D_av = [Dep(), Dep()]
            x1t = [sb(st, "x1D%d" % i, [128, 1024], F32) for i in range(1)] * 2
            D_x1t = [Dep()] * 2
            ds_x1t = [S.dsem()] * 2
            yt = [sb(st, "yD%d" % i, [128, 1024], F32) for i in range(1)] * 2
            D_yt = [Dep()] * 2
            ds_yt = [S.dsem()] * 2
            Gp = [ps(st, "Gp%d" % i, [128, 512], F32) for i in range(2)]
            Vp = [ps(st, "Vp%d" % i, [128, 512], F32) for i in range(2)]
            D_Gp = [Dep(), Dep()]
            D_Vp = [Dep(), Dep()]
            DPp = ps(st, "DPp", [128, 1024], F32)
            D_DPp = Dep()
            D_ss, D_rs = Dep(), Dep()
            D_out = Dep()

            NB = 510
            blocks = []
            s0 = OWN0 - Q0
            while s0 < OWN0 - Q0 + NOWN:
                nb = min(NB, OWN0 - Q0 + NOWN - s0)
                blocks.append((s0, nb))
                s0 += nb

            def load_h(j):
                s0, nb = blocks[j]
                s = j % 2
                S.dma("sp", ds_hs[s], hs[s][:, :, 0:nb + 2], fm(H2T_d)[:, :, s0 - 1:s0 + nb + 1], writes=[D_hs[s]])

            load_h(0)
            xi = 0
            for j, (s0, nb) in enumerate(blocks):
                s = j % 2
                if j + 1 < len(blocks):
                    load_h(j + 1)
                for p in range(22):
                    q = p % 2
                    for (chn, PSt, Dp, acc, Dacc) in ((p, Gp[q], D_Gp[q], ag[q], D_ag[q]),
                                                     (22 + p, Vp[q], D_Vp[q], av[q], D_av[q])):
                        g_ = p // 2
                        Wt = WupA if g_ < 6 else WupB
                        wc0 = (g_ % 6) * 512 + (0 if chn < 22 else 256) + (p % 2) * 128
                        for k in range(8):
                            S.run("pe", lambda e: e.matmul(PSt[:, 0:nb + 2], lhsT=Wt[:, k, wc0:wc0 + 128],
                                                           rhs=hs[s][:, k, 0:nb + 2], start=(k == 0), stop=(k == 7)),
                                  reads=[D_wu[p // 2], D_hs[s]], writes=[Dp], inc=(k == 7))
                        wc = C_FW + chn * 3
                        S.run("act", lambda e: e.activation(out=acc[:, 0:nb], in_=PSt[:, 1:nb + 1], func=AF.Identity,
                                                            bias=pp[:, C_FB + chn:C_FB + chn + 1],
                                                            scale=pp[:, wc + 1:wc + 2]),
                              reads=[Dp, D_const], writes=[Dacc])
                        S.run("dve", lambda e: e.scalar_tensor_tensor(out=acc[:, 0:nb], in0=PSt[:, 0:nb],
                                                                      scalar=pp[:, wc:wc + 1], in1=acc[:, 0:nb],
                                                                      op0=ALU.mult, op1=ALU.add),
                              reads=[Dp, D_const], writes=[Dacc])
                        S.run("dve", lambda e: e.scalar_tensor_tensor(out=acc[:, 0:nb], in0=PSt[:, 2:nb + 2],
                                                                      scalar=pp[:, wc + 2:wc + 3], in1=acc[:, 0:nb],
                                                                      op0=ALU.mult, op1=ALU.add),
                              reads=[Dp, D_const], writes=[Dacc])
                    S.run("act", lambda e: e.activation(out=ag[q][:, 0:nb], in_=ag[q][:, 0:nb], func=AF.Gelu),
                          reads=[], writes=[D_ag[q]])
                    S.run("pool", lambda e: e.tensor_tensor(out=gg[:, p, 0:nb], in0=ag[q][:, 0:nb], in1=av[q][:, 0:nb],
                                                            op=ALU.mult),
                          reads=[D_ag[q], D_av[q]], writes=[D_gg])
                off = 0
                while off < nb:
                    m = min(128, nb - off)
                    xs = xi % 2
                    xi += 1
                    S.dma("sp", ds_x1t[xs], x1t[xs][0:m, :], X1_d[s0 + off:s0 + off + m, :], writes=[D_x1t[xs]])
                    for hf in range(2):
                        for p in range(22):
                            S.run("pe", lambda e: e.matmul(DPp[0:m, hf * 512:(hf + 1) * 512], lhsT=gg[:, p, off:off + m],
                                                           rhs=Wdn[:, p, hf * 512:(hf + 1) * 512],
                                                           start=(p == 0), stop=(p == 21)),
                                  reads=[D_l, D_gg], writes=[D_DPp], inc=(hf == 1 and p == 21))
                    S.run("dve", lambda e: e.tensor_tensor(out=x1t[xs][0:m, :], in0=DPp[0:m, :], in1=x1t[xs][0:m, :],
                                                           op=ALU.add),
                          reads=[D_DPp], writes=[D_x1t[xs]])
                    S.run("act", lambda e: e.activation(out=junk[0:m, :], in_=x1t[xs][0:m, :], func=AF.Square,
                                                        accum_out=stat[0:m, 4:5]),
                          reads=[D_x1t[xs]], writes=[D_junk, D_ss])
                    rstd_chain(stat[0:m, 4:5], stat[0:m, 5:6], D_ss, D_rs)
                    S.run("dve", lambda e: e.scalar_tensor_tensor(out=yt[xs][0:m, :], in0=x1t[xs][0:m, :],
                                                                  scalar=stat[0:m, 5:6], in1=gfb[0:m, :],
                                                                  op0=ALU.mult, op1=ALU.mult),
                          reads=[D_x1t[xs], D_rs, D_l], writes=[D_yt[xs]])
                    o0 = s0 - (OWN0 - Q0) + off
                    S.dma("sp", ds_yt[xs], out_d[o0:o0 + m, :], yt[xs][0:m, :], reads=[D_yt[xs]], writes=[D_out])
                    off += m
            S.barrier()
        stCD.close()
    return nc


def _bias_blocks(rpb):
    qc = np.arange(64)
    kc = np.arange(64)
    cs = np.clip(qc - 8, 0, 48)
    inwin = (kc[:, None] >= cs[None, :]) & (kc[:, None] < cs[None, :] + 16)
    coff = np.clip(kc[:, None] - qc[None, :] + 15, 0, 30)
    blk = rpb[:, :, coff]
    return np.where(inwin[None, None], blk, np.float32(NEGV)).astype(np.float32)


def _table(blk, dvals):
    tb = np.full((2, 64, 8, 2, 64), NEGV, np.float32)
    for kl in range(2):
        for ql in range(2):
            d = dvals[kl][ql]
            if d is not None:
                tb[kl, :, :, ql, :] = blk[:, d].transpose(1, 0, 2)
    return tb.reshape(128, 1024)


def _std_tables(blk):
    out = []
    for c in range(5):
        dv = [[None, None], [None, None]]
        for kl in range(2):
            for ql in range(2):
                rel = 2 * c + kl - ql
                if 0 <= rel <= 7:
                    dv[kl][ql] = rel + 3
        out.append(_table(blk, dv))
    return np.stack(out)


def _edge_tables(blk, half):
    R0 = half * 64
    out = []
    for i in sorted(EDGE_PAIRS):
        for kt in EDGE_PAIRS[i]:
            dv = [[None, None], [None, None]]
            for kl in range(2):
                kr = R0 - 5 + 2 * kt + kl
                for ql in range(2):
                    r = min(max(R0 - 1 + 2 * i + ql, 0), 127)
                    rs = min(max(r - 4, 0), 120)
                    if rs <= kr < rs + 8:
                        dv[kl][ql] = kr - r + 7
            out.append(_table(blk, dv))
    return np.stack(out)


def kernel(x, norm1_g, w_in, b_in, rpb, w_na_out, conv_dw_w, conv_dw_b, conv_ln_g, conv_ln_b,
           w_conv_out, w_out, norm2_g, w_up, ffn_dw_w, ffn_dw_b, w_down, norm_f_g):
    f32 = lambda a: np.ascontiguousarray(np.asarray(a, dtype=np.float32))
    x = f32(x)
    b_in0 = f32(b_in)[0]
    rep = lambda v, n=128: np.ascontiguousarray(np.broadcast_to(f32(v)[None, :], (n, v.shape[-1])))
    col = lambda v: f32(v).reshape(-1, 128).T

    blk = _bias_blocks(f32(rpb)[0])
    tstd = _std_tables(blk)
    tedge = [_edge_tables(blk, 0), _edge_tables(blk, 1)]

    common = {
        "g1b": rep(f32(norm1_g)[0]), "g2b": rep(f32(norm2_g)[0]), "gfb": rep(f32(norm_f_g)),
        "bvb": rep(b_in0[1024:1536]), "tstd": tstd, "idn": np.eye(128, dtype=np.float32),
        "w_in": f32(w_in)[0], "w_na": f32(w_na_out)[0], "w_co": f32(w_conv_out)[0], "w_out": f32(w_out)[0],
        "w_up": f32(w_up)[0], "w_dn": f32(w_down)[0],
    }
    pp_base = np.zeros((128, NPP), np.float32)
    pp_base[:, C_BIN:C_BIN + 36] = col(b_in0)
    cw = f32(conv_dw_w)[0]
    pp_base[:, C_CW:C_CW + 124] = cw.reshape(31, 4, 128).transpose(2, 1, 0).reshape(128, 124)
    pp_base[:, C_CB:C_CB + 4] = col(f32(conv_dw_b)[0])
    pp_base[:, C_LG:C_LG + 4] = col(f32(conv_ln_g)[0])
    pp_base[:, C_LB:C_LB + 4] = col(f32(conv_ln_b)[0])
    fw = f32(ffn_dw_w)[0]
    pp_base[:, C_FW:C_FW + 132] = fw.reshape(3, 44, 128).transpose(2, 1, 0).reshape(128, 132)
    pp_base[:, C_FB:C_FB + 44] = col(f32(ffn_dw_b)[0])

    in_maps = []
    for core in range(8):
        b, half = core // 2, core % 2
        t_lo = half * 4096 - OWN0
        xe = np.zeros((TE, 1024), np.float32)
        lo, hi = max(t_lo, 0), min(t_lo + TE, 8192)
        xe[lo - t_lo:hi - t_lo] = x[b, lo:hi]
        valid = np.zeros(TE, np.float32)
        valid[lo - t_lo:hi - t_lo] = 1.0
        pp = pp_base.copy()
        pp[:, C_VT:C_VT + NT] = valid.reshape(NT, 128).T
        m = dict(common)
        m["x_ext"] = xe
        m["pp"] = pp
        m["vmf"] = np.ascontiguousarray(np.broadcast_to(valid[None, :], (128, TE)))
        m["tedge"] = tedge[half]
        in_maps.append(m)

    nc = build_nc()
    res = run_bass_kernel_spmd(nc, in_maps, core_ids=list(range(8)))
    out = np.empty((4, 8192, 1024), np.float32)
    for core in range(8):
        b, half = core // 2, core % 2
        out[b, half * 4096:(half + 1) * 4096] = res.results[core]["out"]
    return out
```

```python
import numpy as np
from contextlib import ExitStack
import concourse.bass as bass
import concourse.mybir as mybir
from concourse.bass_utils import run_bass_kernel_spmd

F32 = mybir.dt.float32
BF16 = mybir.dt.bfloat16
AF = mybir.ActivationFunctionType
ALU = mybir.AluOpType

NT = 37
TE = NT * 128
Q0 = 256
NQT = 33
NQ = NQT * 128
OWN0 = 320
NOWN = 4096
EPS = 1e-6
NEGV = -30000.0
_STOP = None
_SCRATCH_KIND = "Internal"
_BPAIRS = None
_BDBG = 99
_BHEADS = list(range(8))

C_BIN = 0
C_CW = C_BIN + 36
C_CB = C_CW + 124
C_LG = C_CB + 4
C_LB = C_LG + 4
C_FW = C_LB + 4
C_FB = C_FW + 132
C_VT = C_FB + 44
NPP = C_VT + NT

EDGE_PAIRS = {0: list(range(0, 7)), 1: list(range(1, 7)), 2: list(range(2, 7)),
              30: list(range(30, 35)), 31: list(range(30, 36)), 32: list(range(30, 37))}
N_EDGE = sum(len(v) for v in EDGE_PAIRS.values())


class Dep:
    __slots__ = ("w", "r")

    def __init__(self):
        self.w = {}
        self.r = {}


class DSem:
    def __init__(self, key, sem):
        self.key = key
        self.sem = sem
        self.count = 0


class Sched:
    def __init__(self, nc, stack):
        self.nc = nc
        self.stack = stack
        self.E = {"pe": nc.tensor, "act": nc.scalar, "dve": nc.vector, "pool": nc.gpsimd, "sp": nc.sync}
        self.sems = {}
        self.cnt = {}
        for n in self.E:
            self.sems[n] = stack.enter_context(nc.semaphore("s_" + n))
            self.cnt[n] = 0
        self.seen = {n: {} for n in self.E}
        self.dsems = []
        self.pend = []

    def dsem(self):
        key = "d%d" % len(self.dsems)
        d = DSem(key, self.stack.enter_context(self.nc.semaphore("s_" + key)))
        self.sems[key] = d.sem
        self.dsems.append(d)
        return d

    def _wait(self, eng, key, val):
        if eng == "pe" and key == "pe":
            return
        if self.seen[eng].get(key, 0) >= val:
            return
        self.E[eng].wait_ge(self.sems[key], val)
        self.seen[eng][key] = val

    def _need(self, reads, writes):
        need = {}
        for d in reads:
            for k, v in d.w.items():
                if need.get(k, 0) < v:
                    need[k] = v
        for d in writes:
            for src in (d.w, d.r):
                for k, v in src.items():
                    if need.get(k, 0) < v:
                        need[k] = v
        return need

    @staticmethod
    def _mark(reads, writes, key, val):
        for d in reads:
            if d.r.get(key, 0) < val:
                d.r[key] = val
        for d in writes:
            if d.w.get(key, 0) < val:
                d.w[key] = val

    def run(self, eng, fn, reads=(), writes=(), inc=True):
        for k, v in self._need(reads, writes).items():
            self._wait(eng, k, v)
        ins = fn(self.E[eng])
        if eng == "pe" and not inc:
            self.pend.append((reads, writes))
            return
        self.cnt[eng] += 1
        ins.then_inc(self.sems[eng], 1)
        self._mark(reads, writes, eng, self.cnt[eng])
        if eng == "pe":
            for r, w in self.pend:
                self._mark(r, w, eng, self.cnt[eng])
            self.pend = []

    def dma(self, q, ds, out, in_, reads=(), writes=()):
        if q == "pool":
            ds = self.dsem()
        for k, v in self._need(reads, writes).items():
            self._wait(q, k, v)
        ins = self.E[q].dma_start(out=out, in_=in_)
        ds.count += 16
        ins.then_inc(ds.sem, 16)
        self._mark(reads, writes, ds.key, ds.count)

    def barrier(self):
        for eng in self.E:
            for n in self.E:
                if n != eng and self.cnt[n] > 0:
                    self._wait(eng, n, self.cnt[n])
            for d in self.dsems:
                if d.count > 0:
                    self._wait(eng, d.key, d.count)


def pipeline(nblocks, stages):
    ns = len(stages)
    for t in range(nblocks + ns - 1):
        for st_i in reversed(range(ns)):
            j = t - st_i
            if 0 <= j < nblocks:
                stages[st_i](j)


def build_nc():
    nc = bass.Bass("TRN2", target_bir_lowering=False)
    dt = lambda name, shape, dtype, kind: nc.dram_tensor(name, shape, dtype, kind=kind).ap()
    x_ext = dt("x_ext", [TE, 1024], F32, "ExternalInput")
    pp_d = dt("pp", [128, NPP], F32, "ExternalInput")
    g1b_d = dt("g1b", [128, 1024], F32, "ExternalInput")
    g2b_d = dt("g2b", [128, 1024], F32, "ExternalInput")
    gfb_d = dt("gfb", [128, 1024], F32, "ExternalInput")
    bvb_d = dt("bvb", [128, 512], F32, "ExternalInput")
    vmf_d = dt("vmf", [128, TE], F32, "ExternalInput")
    tstd_d = dt("tstd", [5, 128, 1024], F32, "ExternalInput")
    tedge_d = dt("tedge", [N_EDGE, 128, 1024], F32, "ExternalInput")
    idn_d = dt("idn", [128, 128], F32, "ExternalInput")
    w_in_d = dt("w_in", [1024, 4608], F32, "ExternalInput")
    w_na_d = dt("w_na", [512, 1024], F32, "ExternalInput")
    w_co_d = dt("w_co", [512, 1024], F32, "ExternalInput")
    w_out_d = dt("w_out", [1024, 1024], F32, "ExternalInput")
    w_up_d = dt("w_up", [1024, 5632], F32, "ExternalInput")
    w_dn_d = dt("w_dn", [2816, 1024], F32, "ExternalInput")
    out_d = dt("out", [NOWN, 1024], F32, "ExternalOutput")
    QT_d = dt("QT_s", [512, TE], BF16, _SCRATCH_KIND)
    KT_d = dt("KT_s", [512, TE], BF16, _SCRATCH_KIND)
    V_d = dt("V_s", [TE, 512], BF16, _SCRATCH_KIND)
    U_d = dt("U_s", [512, TE], BF16, _SCRATCH_KIND)
    SA_d = dt("SA_s", [1024, TE], BF16, _SCRATCH_KIND)
    SB_d = dt("SB_s", [1024, TE], BF16, _SCRATCH_KIND)
    BRA_d = dt("BRA_s", [1024, NQ], F32, _SCRATCH_KIND)
    X1_d = dt("X1_s", [NQ, 1024], F32, _SCRATCH_KIND)
    H2T_d = dt("H2T_s", [1024, NQ], BF16, _SCRATCH_KIND)

    fm = lambda ap: ap.rearrange("(c p) t -> p c t", p=128)

    with ExitStack() as top:
        S = Sched(nc, top)
        sb = lambda st, n, s, d: st.enter_context(nc.sbuf_tensor(n, s, d))
        ps = lambda st, n, s, d: st.enter_context(nc.psum_tensor(n, s, d))

        pp = sb(top, "pp_t", [128, NPP], F32)
        idb = sb(top, "idb", [128, 128], BF16)
        ones = sb(top, "ones", [128, 128], BF16)
        onesf = sb(top, "onesf", [128, 128], F32)
        junk = sb(top, "junk", [128, 1024], BF16)
        stat = sb(top, "stat", [128, 8], F32)
        D_const = Dep()
        D_junk = Dep()
        ds_c = S.dsem()
        S.dma("sp", ds_c, pp[:], pp_d, writes=[D_const])
        S.dma("pool", ds_c, idb[:], idn_d, writes=[D_const])
        S.run("dve", lambda e: e.memset(ones[:], 1.0), writes=[D_const])
        S.run("dve", lambda e: e.memset(onesf[:], 1.0 / 512.0), writes=[D_const])

        bq8 = sb(top, "bq8", [128, 4], F32)
        S.run("dve", lambda e: e.tensor_scalar(out=bq8[:], in0=pp[:, C_BIN:C_BIN + 4], scalar1=0.125, scalar2=None,
                                               op0=ALU.mult), reads=[D_const], writes=[D_const])

        def rstd_chain(ssum, outcol, D_ss, D_rs, vt_col=None):
            S.run("dve", lambda e: e.tensor_scalar(out=outcol, in0=ssum, scalar1=1.0 / 1024, scalar2=EPS,
                                                   op0=ALU.mult, op1=ALU.add), reads=[D_ss], writes=[D_rs])
            S.run("act", lambda e: e.activation(out=outcol, in_=outcol, func=AF.Sqrt), reads=[D_rs], writes=[D_rs])
            S.run("dve", lambda e: e.reciprocal(out=outcol, in_=outcol), reads=[D_rs], writes=[D_rs])
            if vt_col is not None:
                S.run("dve", lambda e: e.tensor_scalar(out=outcol, in0=outcol, scalar1=vt_col, scalar2=None,
                                                       op0=ALU.mult), reads=[D_rs, D_const], writes=[D_rs])

        stAB = top.enter_context(ExitStack())
        Wna = sb(stAB, "Wna", [128, 4, 1024], BF16)
        tstd = sb(stAB, "tstd_t", [128, 5, 1024], BF16)
        D_lB = Dep()

        with ExitStack() as st:
            Win = sb(st, "Win", [128, 8, 4608], BF16)
            g1b = sb(st, "g1b_t", [128, 1024], F32)
            bvb = sb(st, "bvb_t", [128, 512], F32)
            D_w = Dep()
            D_wg = [Dep() for _ in range(9)]
            ds_w = S.dsem()
            S.dma("sp", ds_w, g1b[:], g1b_d, writes=[D_w])
            S.dma("sp", ds_w, bvb[:], bvb_d, writes=[D_w])
            w_in_v = w_in_d.rearrange("(k p) n -> p k n", p=128)
            for g in range(9):
                S.dma("pool", ds_w, Win[:, :, g * 512:(g + 1) * 512], w_in_v[:, :, g * 512:(g + 1) * 512], writes=[D_wg[g]])
            S.dma("pool", ds_w, Wna[:], w_na_d.rearrange("(c p) n -> p c n", p=128), writes=[D_lB])
            S.dma("pool", ds_w, tstd[:], tstd_d.rearrange("c p n -> p c n"), writes=[D_lB])

            xt = [sb(st, "xt%d" % i, [128, 1024], F32) for i in range(2)]
            D_xt = [Dep(), Dep()]
            ds_xt = [S.dsem(), S.dsem()]
            hn = [sb(st, "hn%d" % i, [128, 1024], BF16) for i in range(2)]
            D_hn = [Dep(), Dep()]
            hT = [sb(st, "hT%d" % i, [128, 8, 512], BF16) for i in range(2)]
            D_hT = [Dep(), Dep()]
            ptr = [ps(st, "ptrA%d" % i, [128, 8, 128], BF16) for i in range(2)]
            D_ptr = [Dep(), Dep()]
            pacc = [ps(st, "pacc%d" % i, [128, 512], F32) for i in range(4)]
            D_pacc = [Dep() for _ in range(4)]
            pv = [ps(st, "pvA%d" % i, [128, 512], F32) for i in range(2)]
            D_pv = [Dep(), Dep()]
            Qb = [sb(st, "Qb%d" % i, [128, 4, 512], BF16) for i in range(2)]
            Kb = [sb(st, "Kb%d" % i, [128, 4, 512], BF16) for i in range(2)]
            Ub = [sb(st, "Ub%d" % i, [128, 4, 512], BF16) for i in range(2)]
            SAb = [sb(st, "SAb%d" % i, [128, 8, 512], BF16) for i in range(2)]
            SBb = [sb(st, "SBb%d" % i, [128, 8, 512], BF16) for i in range(2)]
            Vb = [sb(st, "Vb%d" % i, [128, 4, 512], BF16) for i in range(2)]
            D_ob = {n: [Dep(), Dep()] for n in ("Q", "K", "U", "SA", "SB", "V")}
            ds_ob = {n: [S.dsem(), S.dsem()] for n in ("Q", "K", "U", "SA", "SB", "V")}
            tma = [sb(st, "tma%d" % i, [128, 512], F32) for i in range(2)]
            tmb = [sb(st, "tmb%d" % i, [128, 512], F32) for i in range(2)]
            D_tma = [Dep(), Dep()]
            D_tmb = [Dep(), Dep()]
            vms = [sb(st, "vms%d" % i, [128, 512], F32) for i in range(2)]
            D_vms = [Dep(), Dep()]
            ds_vms = [S.dsem(), S.dsem()]
            D_ss = Dep()
            D_rs = Dep()
            D_scr = Dep()

            nblk = (NT + 3) // 4
            acc_state = [0]

            def blk(b):
                tiles = list(range(4 * b, min(4 * b + 4, NT)))
                return tiles, 128 * len(tiles), 512 * b, b % 2

            def a_s0(b, ti, part):
                if b >= nblk:
                    return
                tiles, n, t0, bs = blk(b)
                if ti == 0 and part == 0:
                    S.dma("sp", ds_vms[bs], vms[bs][:, 0:n], vmf_d[:, t0:t0 + n], writes=[D_vms[bs]])
                if ti >= len(tiles):
                    return
                t = tiles[ti]
                s_ = t % 2
                if part == 0:
                    S.dma("sp", ds_xt[s_], xt[s_][:], x_ext[t * 128:(t + 1) * 128, :], writes=[D_xt[s_]])
                    S.run("act", lambda e: e.activation(out=junk[:], in_=xt[s_][:], func=AF.Square,
                                                        accum_out=stat[:, 0:1]),
                          reads=[D_xt[s_]], writes=[D_junk, D_ss])
                    rstd_chain(stat[:, 0:1], stat[:, 1:2], D_ss, D_rs)
                    S.run("dve", lambda e: e.scalar_tensor_tensor(out=hn[s_][:], in0=xt[s_][:], scalar=stat[:, 1:2],
                                                                  in1=g1b[:], op0=ALU.mult, op1=ALU.mult),
                          reads=[D_xt[s_], D_rs, D_w], writes=[D_hn[s_]])
                else:
                    for k in range(8):
                        S.run("pe", lambda e: e.transpose(ptr[s_][:, k, :], hn[s_][:, k * 128:(k + 1) * 128], idb[:]),
                              reads=[D_hn[s_], D_const], writes=[D_ptr[s_]], inc=(k == 7))
                    S.run("dve", lambda e: e.tensor_copy(out=hT[bs][:, :, ti * 128:(ti + 1) * 128], in_=ptr[s_][:]),
                          reads=[D_ptr[s_]], writes=[D_hT[bs]])

            def a_s1(b):
                tiles, n, t0, bs = blk(b)

                def proj(fc, evac):
                    a = acc_state[0] % 4
                    acc_state[0] += 1
                    for k in range(8):
                        S.run("pe", lambda e: e.matmul(pacc[a][:, 0:n], lhsT=Win[:, k, fc * 128:(fc + 1) * 128],
                                                       rhs=hT[bs][:, k, 0:n], start=(k == 0), stop=(k == 7)),
                              reads=[D_wg[fc // 4], D_hT[bs]], writes=[D_pacc[a]], inc=(k == 7))
                    evac(pacc[a], D_pacc[a])

                def evac_to(buf, Db, c, func, fc):
                    def f(pa, Dpa):
                        S.run("act", lambda e: e.activation(out=buf[:, c, 0:n], in_=pa[:, 0:n], func=func,
                                                            bias=pp[:, C_BIN + fc:C_BIN + fc + 1]),
                              reads=[Dpa, D_const], writes=[Db])
                    return f

                for c in range(4):
                    def ev_q(pa, Dpa, c=c):
                        S.run("act", lambda e: e.activation(out=Qb[bs][:, c, 0:n], in_=pa[:, 0:n], func=AF.Identity,
                                                            bias=bq8[:, c:c + 1], scale=0.125),
                              reads=[Dpa, D_const], writes=[D_ob["Q"][bs]])
                    proj(c, ev_q)
                a_s0(b + 1, 0, 0)
                for c in range(4):
                    proj(4 + c, evac_to(Kb[bs], D_ob["K"][bs], c, AF.Identity, 4 + c))
                a_s0(b + 1, 0, 1)
                a_s0(b + 1, 1, 0)
                for ti, t in enumerate(tiles):
                    v = ti % 2
                    for k in range(8):
                        S.run("pe", lambda e: e.matmul(pv[v][:, 0:512], lhsT=hT[bs][:, k, ti * 128:(ti + 1) * 128],
                                                       rhs=Win[:, k, 1024:1536], start=(k == 0), stop=(k == 7)),
                              reads=[D_wg[2], D_hT[bs]], writes=[D_pv[v]], inc=(k == 7))
                    S.run("dve", lambda e: e.tensor_tensor(out=Vb[bs][:, ti, :], in0=pv[v][:], in1=bvb[:], op=ALU.add),
                          reads=[D_pv[v], D_w], writes=[D_ob["V"][bs]])
                for c in range(4):
                    g = c % 2
                    fa, fb = 12 + c, 16 + c

                    def ev_a(pa, Dpa):
                        S.run("act", lambda e: e.activation(out=tma[g][:, 0:n], in_=pa[:, 0:n], func=AF.Identity,
                                                            bias=pp[:, C_BIN + fa:C_BIN + fa + 1]),
                              reads=[Dpa, D_const], writes=[D_tma[g]])

                    def ev_b(pa, Dpa):
                        S.run("act", lambda e: e.activation(out=tmb[g][:, 0:n], in_=pa[:, 0:n], func=AF.Sigmoid,
                                                            bias=pp[:, C_BIN + fb:C_BIN + fb + 1]),
                              reads=[Dpa, D_const], writes=[D_tmb[g]])
                    proj(fa, ev_a)
                    proj(fb, ev_b)
                    S.run("dve", lambda e: e.tensor_tensor(out=tma[g][:, 0:n], in0=tma[g][:, 0:n], in1=tmb[g][:, 0:n],
                                                           op=ALU.mult),
                          reads=[D_tmb[g]], writes=[D_tma[g]])
                    S.run("pool", lambda e: e.tensor_tensor(out=Ub[bs][:, c, 0:n], in0=tma[g][:, 0:n],
                                                            in1=vms[bs][:, 0:n], op=ALU.mult),
                          reads=[D_tma[g], D_vms[bs]], writes=[D_ob["U"][bs]])
                a_s0(b + 1, 1, 1)
                a_s0(b + 1, 2, 0)
                for c in range(8):
                    proj(20 + c, evac_to(SAb[bs], D_ob["SA"][bs], c, AF.Sigmoid, 20 + c))
                a_s0(b + 1, 2, 1)
                a_s0(b + 1, 3, 0)
                for c in range(8):
                    proj(28 + c, evac_to(SBb[bs], D_ob["SB"][bs], c, AF.Sigmoid, 28 + c))
                a_s0(b + 1, 3, 1)
                q = "sp"
                S.dma(q, ds_ob["Q"][bs], fm(QT_d)[:, :, t0:t0 + n], Qb[bs][:, :, 0:n], reads=[D_ob["Q"][bs]], writes=[D_scr])
                S.dma(q, ds_ob["K"][bs], fm(KT_d)[:, :, t0:t0 + n], Kb[bs][:, :, 0:n], reads=[D_ob["K"][bs]], writes=[D_scr])
                S.dma(q, ds_ob["V"][bs],
                      V_d[t0:t0 + n, :].rearrange("(t p) n -> p t n", p=128), Vb[bs][:, 0:len(tiles), :],
                      reads=[D_ob["V"][bs]], writes=[D_scr])
                S.dma(q, ds_ob["U"][bs], fm(U_d)[:, :, t0:t0 + n], Ub[bs][:, :, 0:n], reads=[D_ob["U"][bs]], writes=[D_scr])
                S.dma(q, ds_ob["SA"][bs], fm(SA_d)[:, :, t0:t0 + n], SAb[bs][:, :, 0:n], reads=[D_ob["SA"][bs]], writes=[D_scr])
                S.dma(q, ds_ob["SB"][bs], fm(SB_d)[:, :, t0:t0 + n], SBb[bs][:, :, 0:n], reads=[D_ob["SB"][bs]], writes=[D_scr])

            for ti in range(4):
                a_s0(0, ti, 0)
                a_s0(0, ti, 1)
            for b in range(nblk):
                a_s1(b)
            S.barrier()
        if _STOP == "A":
            return nc

        with ExitStack() as st:
            KTs = sb(st, "KTs", [128, 4, TE], BF16)
            Vs = sb(st, "Vs", [128, NT, 512], BF16)
            D_l = D_lB
            ds_l = S.dsem()
            v_v = V_d.rearrange("(t p) n -> p t n", p=128)
            NKG = (NT + 7) // 8
            D_kv = [Dep() for _ in range(NKG)]
            ds_kv = [S.dsem() for _ in range(NKG)]
            def load_kv(g):
                t, te = 8 * g, min(8 * g + 8, NT)
                S.dma("sp", ds_kv[g], KTs[:, :, t * 128:te * 128], fm(KT_d)[:, :, t * 128:te * 128], writes=[D_kv[g]])
                S.dma("sp", ds_kv[g], Vs[:, t:te, :], v_v[:, t:te, :], writes=[D_kv[g]])

            load_kv(0)

            NEB = 4
            eb = [sb(st, "eb%d" % i, [128, 1024], BF16) for i in range(NEB)]
            eb32 = [sb(st, "eb32_%d" % i, [128, 1024], F32) for i in range(NEB)]
            D_eb32 = [Dep() for _ in range(NEB)]
            D_eb = [Dep() for _ in range(NEB)]
            ds_eb = [S.dsem() for _ in range(NEB)]
            Spl = [ps(st, "Sp%d" % i, [128, 1024], F32) for i in range(2)]
            D_Spl = [Dep(), Dep()]
            Op = ps(st, "Op", [128, 1024], F32)
            SUp = ps(st, "SUp", [128, 1024], F32)
            NAp = SUp[:].rearrange("p (a b) -> p a b", a=8)
            D_Op, D_SUp = Dep(), Dep()
            D_NAp = D_SUp
            rcp = sb(st, "rcp", [128, 1024], F32)
            D_rcp = Dep()
            nab = [sb(st, "nab%d" % i, [128, 8, 128], F32) for i in range(2)]
            D_nab = [Dep(), Dep()]
            ds_nab = [S.dsem(), S.dsem()]
            D_bra = Dep()
            Qm = [sb(st, "Qm%d" % i, [128, 4, 2, 128], BF16) for i in range(2)]
            D_Qm = [Dep(), Dep()]
            ds_Qm = [S.dsem(), S.dsem()]
            SAt = [sb(st, "SAt%d" % i, [128, 8, 128], BF16) for i in range(3)]
            D_SAt = [Dep(), Dep(), Dep()]
            ds_SAt = [S.dsem(), S.dsem(), S.dsem()]
            for i in range(2):
                S.run("dve", lambda e: e.memset(Qm[i][:], 0.0), writes=[D_Qm[i]])
            OTz = sb(st, "OTz", [128, 4, 128], BF16)
            D_OTz = Dep()
            Op4 = Op[:].rearrange("p (a b c) -> p a b c", a=4, b=2)
            rcp4 = rcp[:].rearrange("p (a b c) -> p a b c", a=4, b=2)
            NPT = 7
            PT = [[sb(st, "PT%d_%d" % (g, i), [128, 1024], BF16) for i in range(NPT)] for g in range(2)]
            D_PT = [[Dep() for _ in range(NPT)] for g in range(2)]

            def load_edge(idx):
                sl = idx % NEB
                S.dma("sp", ds_eb[sl], eb32[sl][:], tedge_d[idx], writes=[D_eb32[sl]])
                if idx % 2 == 0:
                    S.run("act", lambda e: e.copy(out=eb[sl][:], in_=eb32[sl][:]), reads=[D_eb32[sl]], writes=[D_eb[sl]])
                else:
                    S.run("dve", lambda e: e.tensor_copy(out=eb[sl][:], in_=eb32[sl][:]), reads=[D_eb32[sl]],
                          writes=[D_eb[sl]])

            def load_q(i):
                qs = i % 2
                qt = i + 2
                S.dma("sp", ds_Qm[qs], Qm[qs][0:64, :, 0, :], fm(QT_d)[0:64, :, qt * 128:(qt + 1) * 128], writes=[D_Qm[qs]])
                S.dma("sp", ds_Qm[qs], Qm[qs][64:128, :, 1, :], fm(QT_d)[64:128, :, qt * 128:(qt + 1) * 128],
                      writes=[D_Qm[qs]])
                S.dma("sp", ds_SAt[i % 3], SAt[i % 3][:], fm(SA_d)[:, :, qt * 128:(qt + 1) * 128], writes=[D_SAt[i % 3]])

            pair_kts = [EDGE_PAIRS.get(i, list(range(i, i + 5))) for i in range(NQT)]
            edge_base = {}
            eacc = 0
            for i in range(NQT):
                if i in EDGE_PAIRS:
                    edge_base[i] = eacc
                    eacc += len(pair_kts[i])
            for idx in range(NEB - 1):
                load_edge(idx)
            load_q(0)
            kv_issue = {1: 1, 6: 2, 14: 3, 22: 4}
            cstate = [0]

            def s_chunk(i, j):
                qt = i + 2
                qs = i % 2
                g = i % 2
                kt = pair_kts[i][j]
                cs = cstate[0] % 2
                cstate[0] += 1
                Sp, D_Sp = Spl[cs], D_Spl[cs]
                if i in EDGE_PAIRS:
                    ei = edge_base[i] + j
                    nxt = ei + NEB - 1
                    if nxt < N_EDGE:
                        load_edge(nxt)
                    tb = eb[ei % NEB][:]
                    D_tb = D_eb[ei % NEB]
                else:
                    tb = tstd[:, j, :]
                    D_tb = D_l
                for hf in range(2):
                    S.run("pe", lambda e: e.matmul(Sp[:, hf * 512:(hf + 1) * 512], lhsT=idb[:],
                                                   rhs=tb[:, hf * 512:(hf + 1) * 512], start=True, stop=False),
                          reads=[D_const, D_tb], writes=[D_Sp], inc=False)
                    for h in range(4 * hf, 4 * hf + 4):
                        S.run("pe", lambda e: e.matmul(Sp[:, h * 128:(h + 1) * 128],
                                                       lhsT=KTs[:, h // 2, kt * 128:(kt + 1) * 128],
                                                       rhs=Qm[qs][:, h // 2, h % 2, :], start=False, stop=(h % 4 == 3)),
                              reads=[D_kv[kt // 8], D_Qm[qs]], writes=[D_Sp], inc=(h == 7))
                S.run("act", lambda e: e.activation(out=PT[g][j][:], in_=Sp[:], func=AF.Exp),
                      reads=[D_Sp], writes=[D_PT[g][j]])

            def pv_heads(i, heads):
                g = i % 2
                kts = pair_kts[i]
                nk = len(kts)
                for h in heads:
                    for j, kt in enumerate(kts):
                        S.run("pe", lambda e: e.matmul(Op[:, h * 128:(h + 1) * 128],
                                                       lhsT=Vs[:, kt, (h // 2) * 128:(h // 2 + 1) * 128],
                                                       rhs=PT[g][j][:, h * 128:(h + 1) * 128],
                                                       start=(j == 0), stop=(j == nk - 1)),
                              reads=[D_kv[kt // 8], D_PT[g][j]], writes=[D_Op], inc=(h == 7 and j == nk - 1))

            def tail(i, part):
                qs = i % 2
                g = i % 2
                nk = len(pair_kts[i])
                if part == 0:
                    tail_a(i, g, nk)
                else:
                    tail_b(i, qs)

            def tail_a(i, g, nk):
                for hf in range(2):
                    for j in range(nk):
                        S.run("pe", lambda e: e.matmul(SUp[:, hf * 512:(hf + 1) * 512], lhsT=ones[:, 0:128],
                                                       rhs=PT[g][j][:, hf * 512:(hf + 1) * 512],
                                                       start=(j == 0), stop=(j == nk - 1)),
                              reads=[D_const, D_PT[g][j]], writes=[D_SUp], inc=(hf == 1 and j == nk - 1))
                ns = i % 2
                S.run("act", lambda e: e.activation(out=rcp[:], in_=SUp[:], func=AF.Ln), reads=[D_SUp], writes=[D_rcp])
                S.run("act", lambda e: e.activation(out=rcp[:], in_=rcp[:], func=AF.Exp, scale=-1.0),
                      reads=[], writes=[D_rcp])
                S.run("dve", lambda e: e.tensor_tensor(out=OTz[0:64, :, :], in0=Op4[0:64, :, 0, :],
                                                       in1=rcp4[0:64, :, 0, :], op=ALU.mult),
                      reads=[D_Op, D_rcp], writes=[D_OTz])
                S.run("dve", lambda e: e.tensor_tensor(out=OTz[64:128, :, :], in0=Op4[64:128, :, 1, :],
                                                       in1=rcp4[64:128, :, 1, :], op=ALU.mult),
                      reads=[D_Op, D_rcp], writes=[D_OTz])

            def tail_b(i, qs):
                ns = i % 2
                for dc in range(8):
                    for c in range(4):
                        S.run("pe", lambda e: e.matmul(NAp[:, dc, :], lhsT=Wna[:, c, dc * 128:(dc + 1) * 128],
                                                       rhs=OTz[:, c, :], start=(c == 0), stop=(c == 3)),
                              reads=[D_l, D_OTz], writes=[D_NAp], inc=(dc == 7 and c == 3))
                S.run("dve", lambda e: e.tensor_tensor(out=nab[ns][:], in0=NAp, in1=SAt[i % 3][:], op=ALU.mult),
                      reads=[D_NAp, D_SAt[i % 3]], writes=[D_nab[ns]])
                S.dma("sp", ds_nab[ns], fm(BRA_d)[:, :, i * 128:(i + 1) * 128], nab[ns][:],
                      reads=[D_nab[ns]], writes=[D_bra])

            for i in range(NQT + 1):
                if i in kv_issue and kv_issue[i] < NKG:
                    load_kv(kv_issue[i])
                if i + 1 < NQT:
                    load_q(i + 1)
                nk = len(pair_kts[i]) if i < NQT else 0
                hsplit = [[0, 1], [2, 3], [4, 5], [6, 7]]
                for j in range(max(nk, 5)):
                    if j < nk:
                        s_chunk(i, j)
                    if i >= 1 and j < 4:
                        pv_heads(i - 1, hsplit[j])
                    if i >= 1 and j == 3:
                        tail(i - 1, 0)
                if i >= 1:
                    tail(i - 1, 1)
            S.barrier()
        stAB.close()
        if _STOP == "B":
            return nc

        stCD = top.enter_context(ExitStack())
        WupA = sb(stCD, "WupA", [128, 8, 3072], BF16)
        D_wu = [Dep() for _ in range(11)]
        w_up_v = w_up_d.rearrange("(k p) n -> p k n", p=128)

        def load_wup(g, W, gl, ds_):
            S.dma("pool", ds_, W[:, :, gl * 512:gl * 512 + 256], w_up_v[:, :, (2 * g) * 128:(2 * g + 2) * 128],
                  writes=[D_wu[g]])
            S.dma("pool", ds_, W[:, :, gl * 512 + 256:gl * 512 + 512],
                  w_up_v[:, :, (22 + 2 * g) * 128:(22 + 2 * g + 2) * 128], writes=[D_wu[g]])

        with ExitStack() as st:
            DG = sb(st, "DG", [128, 4, 31, 128], BF16)
            D_DGc = [Dep() for _ in range(4)]
            Wco = sb(st, "Wco", [128, 4, 1024], BF16)
            Wout = sb(st, "Wout", [128, 8, 1024], BF16)
            g2b = sb(st, "g2b_t", [128, 1024], F32)
            D_l = Dep()
            ds_l = S.dsem()
            S.dma("pool", ds_l, Wco[:], w_co_d.rearrange("(c p) n -> p c n", p=128), writes=[D_l])
            S.dma("pool", ds_l, Wout[:], w_out_d.rearrange("(c p) n -> p c n", p=128), writes=[D_l])
            S.dma("sp", ds_l, g2b[:], g2b_d, writes=[D_l])
            for c in range(4):
                for k in range(31):
                    col = C_CW + c * 31 + k
                    if k % 2 == 0:
                        S.run("dve", lambda e: e.tensor_scalar(out=DG[:, c, k, :], in0=idb[:], scalar1=pp[:, col:col + 1],
                                                               scalar2=None, op0=ALU.mult),
                              reads=[D_const], writes=[D_DGc[c]])
                    else:
                        S.run("act", lambda e: e.activation(out=DG[:, c, k, :], in_=idb[:], func=AF.Copy,
                                                            scale=pp[:, col:col + 1]),
                              reads=[D_const], writes=[D_DGc[c]])
            for g in range(6):
                load_wup(g, WupA, g, ds_l)

            BN = 256
            Us = [sb(st, "Us%d" % i, [128, 4, BN + 30], BF16) for i in range(2)]
            D_Us = [Dep(), Dep()]
            ds_Us = [S.dsem(), S.dsem()]
            MAs = [sb(st, "MAs%d" % i, [128, 8, BN], F32) for i in range(2)]
            SBs = [sb(st, "SBs%d" % i, [128, 8, BN], BF16) for i in range(2)]
            D_in = [Dep(), Dep()]
            ds_in = [S.dsem(), S.dsem()]
            cv = [sb(st, "cv%d" % i, [128, 4, BN], F32) for i in range(2)]
            D_cv = [Dep(), Dep()]
            cvs = sb(st, "cvs", [128, 4, BN], BF16)
            onesb = sb(st, "onesb", [128, 128], BF16)
            D_cvs = Dep()
            S.run("dve", lambda e: e.memset(onesb[:], 1.0 / 512.0), writes=[D_cvs])
            mean_s = sb(st, "mean_s", [128, BN], F32)
            rstd_s = sb(st, "rstd_s", [128, BN], F32)
            D_mean, D_rstd = Dep(), Dep()
            ua = [sb(st, "ua%d" % i, [128, 4, BN], BF16) for i in range(2)]
            D_ua = [Dep(), Dep()]
            t2 = [sb(st, "t2_%d" % i, [128, BN], F32) for i in range(2)]
            D_t2 = [Dep(), Dep()]
            mg = [sb(st, "mg%d" % i, [128, 8, BN], BF16) for i in range(2)]
            D_mg = [Dep(), Dep()]
            xt = [sb(st, "xtC%d" % i, [128, 1024], F32) for i in range(4)]
            D_xt = [Dep() for _ in range(4)]
            ds_xt = [S.dsem() for _ in range(4)]
            x1 = [sb(st, "x1C%d" % i, [128, 1024], F32) for i in range(2)]
            D_x1 = [Dep(), Dep()]
            ds_x1 = [S.dsem(), S.dsem()]
            h2b = [sb(st, "h2b%d" % i, [128, 2, 1024], BF16) for i in range(2)]
            D_h2b = [Dep(), Dep()]
            h2T = [sb(st, "h2T%d" % i, [128, 8, 128], BF16) for i in range(2)]
            D_h2T = [Dep(), Dep()]
            ds_h2T = [S.dsem(), S.dsem()]
            CVp = [ps(st, "CVp%d" % i, [128, 512], F32) for i in range(2)]
            D_CVp = [Dep(), Dep()]
            STp = ps(st, "STp", [128, 512], F32)
            D_STp = Dep()
            BRp = [ps(st, "BRp%d" % i, [128, 512], F32) for i in range(2)]
            D_BRp = [Dep(), Dep()]
            OPp = [ps(st, "OPp%d" % i, [128, 512], F32) for i in range(2)]
            D_OPp = [Dep(), Dep()]
            PTp = ps(st, "PTp", [128, 8, 128], BF16)
            D_PTp = Dep()
            D_ss, D_rs = Dep(), Dep()
            D_scr = Dep()

            nblk = (NQT + 1) // 2
            blocks = []
            for j in range(nblk):
                nt_ = min(2, NQT - 2 * j)
                blocks.append((Q0 + BN * j, 128 * nt_, nt_))

            def load_us(j):
                t0, n, _ = blocks[j]
                s_ = j % 2
                S.dma("sp", ds_Us[s_], Us[s_][:, :, 0:n + 30], fm(U_d)[:, :, t0 - 15:t0 + n + 15], writes=[D_Us[s_]])

            def c_s0(j):
                t0, n, nt_ = blocks[j]
                s_ = j % 2
                if j + 1 < nblk:
                    load_us(j + 1)
                for c in range(4):
                    p = c % 2
                    for k in range(31):
                        S.run("pe", lambda e: e.matmul(CVp[p][:, 0:n], lhsT=DG[:, c, k, :], rhs=Us[s_][:, c, k:k + n],
                                                       start=(k == 0), stop=(k == 30)),
                              reads=[D_DGc[c], D_Us[s_]], writes=[D_CVp[p]], inc=(k == 30))
                    S.run("act", lambda e: e.activation(out=cv[s_][:, c, 0:n], in_=CVp[p][:, 0:n], func=AF.Identity,
                                                        bias=pp[:, C_CB + c:C_CB + c + 1]),
                          reads=[D_CVp[p], D_const], writes=[D_cv[s_]])

            def c_s1(j):
                t0, n, nt_ = blocks[j]
                s_ = j % 2
                S.dma("sp", ds_in[s_], MAs[s_][:, :, 0:n], fm(BRA_d)[:, :, t0 - Q0:t0 - Q0 + n], writes=[D_in[s_]])
                S.dma("sp", ds_in[s_], SBs[s_][:, :, 0:n], fm(SB_d)[:, :, t0:t0 + n], writes=[D_in[s_]])
                S.run("dve", lambda e: e.tensor_tensor(out=cvs[:, :, 0:n], in0=cv[s_][:, :, 0:n], in1=cv[s_][:, :, 0:n],
                                                       op=ALU.mult), reads=[D_cv[s_]], writes=[D_cvs])
                for c in range(4):
                    S.run("pe", lambda e: e.matmul(STp[:, 0:n], lhsT=onesf[:], rhs=cv[s_][:, c, 0:n],
                                                   start=(c == 0), stop=(c == 3)),
                          reads=[D_const, D_cv[s_]], writes=[D_STp], inc=(c == 3))
                for c in range(4):
                    S.run("pe", lambda e: e.matmul(STp[:, 256:256 + n], lhsT=onesb[:], rhs=cvs[:, c, 0:n],
                                                   start=(c == 0), stop=(c == 3)),
                          reads=[D_const, D_cvs], writes=[D_STp], inc=(c == 3))
                S.run("act", lambda e: e.copy(out=mean_s[:, 0:n], in_=STp[:, 0:n]), reads=[D_STp], writes=[D_mean])
                S.run("dve", lambda e: e.tensor_tensor(out=rstd_s[:, 0:n], in0=mean_s[:, 0:n], in1=mean_s[:, 0:n],
                                                       op=ALU.mult), reads=[D_mean], writes=[D_rstd])
                S.run("dve", lambda e: e.tensor_tensor(out=rstd_s[:, 0:n], in0=STp[:, 256:256 + n], in1=rstd_s[:, 0:n],
                                                       op=ALU.subtract), reads=[D_STp], writes=[D_rstd])
                S.run("dve", lambda e: e.tensor_scalar(out=rstd_s[:, 0:n], in0=rstd_s[:, 0:n], scalar1=0.0, scalar2=EPS,
                                                       op0=ALU.max, op1=ALU.add), reads=[], writes=[D_rstd])
                S.run("act", lambda e: e.activation(out=rstd_s[:, 0:n], in_=rstd_s[:, 0:n], func=AF.Sqrt),
                      reads=[], writes=[D_rstd])
                S.run("dve", lambda e: e.reciprocal(out=rstd_s[:, 0:n], in_=rstd_s[:, 0:n]), reads=[], writes=[D_rstd])
                for c in range(4):
                    S.run("dve", lambda e: e.tensor_tensor(out=cv[s_][:, c, 0:n], in0=cv[s_][:, c, 0:n],
                                                           in1=mean_s[:, 0:n], op=ALU.subtract),
                          reads=[D_mean, D_STp], writes=[D_cv[s_]])
                    S.run("dve", lambda e: e.tensor_tensor(out=cv[s_][:, c, 0:n], in0=cv[s_][:, c, 0:n],
                                                           in1=rstd_s[:, 0:n], op=ALU.mult),
                          reads=[D_rstd], writes=[D_cv[s_]])
                    S.run("act", lambda e: e.activation(out=ua[s_][:, c, 0:n], in_=cv[s_][:, c, 0:n], func=AF.Silu,
                                                        bias=pp[:, C_LB + c:C_LB + c + 1],
                                                        scale=pp[:, C_LG + c:C_LG + c + 1]),
                          reads=[D_cv[s_], D_const], writes=[D_ua[s_]])

            def c_s2(j):
                t0, n, nt_ = blocks[j]
                s_ = j % 2
                for tt in range(nt_):
                    xq = (j % 2) * 2 + tt
                    S.dma("sp", ds_xt[xq], xt[xq][:], x_ext[t0 + tt * 128:t0 + (tt + 1) * 128, :], writes=[D_xt[xq]])
                for dc in range(8):
                    p = dc % 2
                    for c in range(4):
                        S.run("pe", lambda e: e.matmul(BRp[p][:, 0:n], lhsT=Wco[:, c, dc * 128:(dc + 1) * 128],
                                                       rhs=ua[s_][:, c, 0:n], start=(c == 0), stop=(c == 3)),
                              reads=[D_l, D_ua[s_]], writes=[D_BRp[p]], inc=(c == 3))
                    S.run("dve", lambda e: e.tensor_tensor(out=t2[p][:, 0:n], in0=BRp[p][:, 0:n], in1=SBs[s_][:, dc, 0:n],
                                                           op=ALU.mult), reads=[D_BRp[p], D_in[s_]], writes=[D_t2[p]])
                    S.run("pool", lambda e: e.tensor_tensor(out=mg[s_][:, dc, 0:n], in0=MAs[s_][:, dc, 0:n],
                                                            in1=t2[p][:, 0:n], op=ALU.add),
                          reads=[D_in[s_], D_t2[p]], writes=[D_mg[s_]])

            def c_s3(j):
                t0, n, nt_ = blocks[j]
                s_ = j % 2
                for tt in range(nt_):
                    xs = tt
                    tile_i = (t0 // 128) + tt
                    xq = (j % 2) * 2 + tt
                    for hf in range(2):
                        for k in range(8):
                            S.run("pe", lambda e: e.matmul(OPp[hf][:, 0:512], lhsT=mg[s_][:, k, tt * 128:(tt + 1) * 128],
                                                           rhs=Wout[:, k, hf * 512:(hf + 1) * 512],
                                                           start=(k == 0), stop=(k == 7)),
                                  reads=[D_l, D_mg[s_]], writes=[D_OPp[hf]], inc=(k == 7))
                        S.run("dve", lambda e: e.tensor_tensor(out=x1[xs][:, hf * 512:(hf + 1) * 512], in0=OPp[hf][:, 0:512],
                                                               in1=xt[xq][:, hf * 512:(hf + 1) * 512], op=ALU.add),
                              reads=[D_OPp[hf], D_xt[xq]], writes=[D_x1[xs]])
                    S.dma("sp", ds_x1[xs], X1_d[t0 - Q0 + tt * 128:t0 - Q0 + (tt + 1) * 128, :], x1[xs][:],
                          reads=[D_x1[xs]], writes=[D_scr])
                    S.run("act", lambda e: e.activation(out=junk[:], in_=x1[xs][:], func=AF.Square,
                                                        accum_out=stat[:, 2:3]),
                          reads=[D_x1[xs]], writes=[D_junk, D_ss])
                    rstd_chain(stat[:, 2:3], stat[:, 3:4], D_ss, D_rs, vt_col=pp[:, C_VT + tile_i:C_VT + tile_i + 1])
                    S.run("dve", lambda e: e.scalar_tensor_tensor(out=h2b[s_][:, tt, :], in0=x1[xs][:], scalar=stat[:, 3:4],
                                                                  in1=g2b[:], op0=ALU.mult, op1=ALU.mult),
                          reads=[D_x1[xs], D_rs, D_l], writes=[D_h2b[s_]])

            def c_s4(j):
                t0, n, nt_ = blocks[j]
                s_ = j % 2
                for tt in range(nt_):
                    xs = tt
                    for k in range(8):
                        S.run("pe", lambda e: e.transpose(PTp[:, k, :], h2b[s_][:, tt, k * 128:(k + 1) * 128], idb[:]),
                              reads=[D_h2b[s_], D_const], writes=[D_PTp], inc=(k == 7))
                    S.run("act", lambda e: e.copy(out=h2T[xs][:], in_=PTp[:]), reads=[D_PTp], writes=[D_h2T[xs]])
                    c0_ = t0 - Q0 + tt * 128
                    S.dma("sp", ds_h2T[xs], fm(H2T_d)[:, :, c0_:c0_ + 128], h2T[xs][:], reads=[D_h2T[xs]], writes=[D_scr])

            load_us(0)
            pipeline(nblk, [c_s0, c_s1, c_s2, c_s3, c_s4])
            S.barrier()
        if _STOP == "C":
            return nc

        with ExitStack() as st:
            WupB = sb(st, "WupB", [128, 8, 2560], BF16)
            Wdn = sb(st, "Wdn", [128, 22, 1024], BF16)
            gfb = sb(st, "gfb_t", [128, 1024], F32)
            D_l = Dep()
            ds_l = S.dsem()
            for g in range(6, 11):
                load_wup(g, WupB, g - 6, ds_l)
            w_dn_v = w_dn_d.rearrange("(k p) n -> p k n", p=128)
            for k in range(0, 22, 6):
                ke = min(k + 6, 22)
                S.dma("pool", ds_l, Wdn[:, k:ke, :], w_dn_v[:, k:ke, :], writes=[D_l])
            S.dma("sp", ds_l, gfb[:], gfb_d, writes=[D_l])

            hs = [sb(st, "hsD%d" % i, [128, 8, 512], BF16) for i in range(2)]
            D_hs = [Dep(), Dep()]
            ds_hs = [S.dsem(), S.dsem()]
            gg = sb(st, "gg", [128, 22, 512], BF16)
            D_gg = Dep()
            ag = [sb(st, "ag%d" % i, [128, 512], F32) for i in range(2)]
            av = [sb(st, "av%d" % i, [128, 512], F32) for i in range(2)]
            D_ag = [Dep(), Dep()]
            D_av = [Dep(), Dep()]
            x1t = [sb(st, "x1D%d" % i, [128, 1024], F32) for i in range(1)] * 2
            D_x1t = [Dep()] * 2
            ds_x1t = [S.dsem()] * 2
            yt = [sb(st, "yD%d" % i, [128, 1024], F32) for i in range(1)] * 2
            D_yt = [Dep()] * 2
            ds_yt = [S.dsem()] * 2
            Gp = [ps(st, "Gp%d" % i, [128, 512], F32) for i in range(2)]
            Vp = [ps(st, "Vp%d" % i, [128, 512], F32) for i in range(2)]
            D_Gp = [Dep(), Dep()]
            D_Vp = [Dep(), Dep()]
            DPp = ps(st, "DPp", [128, 1024], F32)
            D_DPp = Dep()
            D_ss, D_rs = Dep(), Dep()
            D_out = Dep()

            NB = 510
            blocks = []
            s0 = OWN0 - Q0
            while s0 < OWN0 - Q0 + NOWN:
                nb = min(NB, OWN0 - Q0 + NOWN - s0)
                blocks.append((s0, nb))
                s0 += nb

            def load_h(j):
                s0, nb = blocks[j]
                s = j % 2
                S.dma("sp", ds_hs[s], hs[s][:, :, 0:nb + 2], fm(H2T_d)[:, :, s0 - 1:s0 + nb + 1], writes=[D_hs[s]])

            load_h(0)
            xi = 0
            for j, (s0, nb) in enumerate(blocks):
                s = j % 2
                if j + 1 < len(blocks):
                    load_h(j + 1)
                for p in range(22):
                    q = p % 2
                    for (chn, PSt, Dp, acc, Dacc) in ((p, Gp[q], D_Gp[q], ag[q], D_ag[q]),
                                                     (22 + p, Vp[q], D_Vp[q], av[q], D_av[q])):
                        g_ = p // 2
                        Wt = WupA if g_ < 6 else WupB
                        wc0 = (g_ % 6) * 512 + (0 if chn < 22 else 256) + (p % 2) * 128
                        for k in range(8):
                            S.run("pe", lambda e: e.matmul(PSt[:, 0:nb + 2], lhsT=Wt[:, k, wc0:wc0 + 128],
                                                           rhs=hs[s][:, k, 0:nb + 2], start=(k == 0), stop=(k == 7)),
                                  reads=[D_wu[p // 2], D_hs[s]], writes=[Dp], inc=(k == 7))
                        wc = C_FW + chn * 3
                        S.run("act", lambda e: e.activation(out=acc[:, 0:nb], in_=PSt[:, 1:nb + 1], func=AF.Identity,
                                                            bias=pp[:, C_FB + chn:C_FB + chn + 1],
                                                            scale=pp[:, wc + 1:wc + 2]),
                              reads=[Dp, D_const], writes=[Dacc])
                        S.run("dve", lambda e: e.scalar_tensor_tensor(out=acc[:, 0:nb], in0=PSt[:, 0:nb],
                                                                      scalar=pp[:, wc:wc + 1], in1=acc[:, 0:nb],
                                                                      op0=ALU.mult, op1=ALU.add),
                              reads=[Dp, D_const], writes=[Dacc])
                        S.run("dve", lambda e: e.scalar_tensor_tensor(out=acc[:, 0:nb], in0=PSt[:, 2:nb + 2],
                                                                      scalar=pp[:, wc + 2:wc + 3], in1=acc[:, 0:nb],
                                                                      op0=ALU.mult, op1=ALU.add),
                              reads=[Dp, D_const], writes=[Dacc])
                    S.run("act", lambda e: e.activation(out=ag[q][:, 0:nb], in_=ag[q][:, 0:nb], func=AF.Gelu),
                          reads=[], writes=[D_ag[q]])
                    S.run("pool", lambda e: e.tensor_tensor(out=gg[:, p, 0:nb], in0=ag[q][:, 0:nb], in1=av[q][:, 0:nb],
                                                            op=ALU.mult),
                          reads=[D_ag[q], D_av[q]], writes=[D_gg])
                off = 0
                while off < nb:
                    m = min(128, nb - off)
                    xs = xi % 2
                    xi += 1
                    S.dma("sp", ds_x1t[xs], x1t[xs][0:m, :], X1_d[s0 + off:s0 + off + m, :], writes=[D_x1t[xs]])
                    for hf in range(2):
                        for p in range(22):
                            S.run("pe", lambda e: e.matmul(DPp[0:m, hf * 512:(hf + 1) * 512], lhsT=gg[:, p, off:off + m],
                                                           rhs=Wdn[:, p, hf * 512:(hf + 1) * 512],
                                                           start=(p == 0), stop=(p == 21)),
                                  reads=[D_l, D_gg], writes=[D_DPp], inc=(hf == 1 and p == 21))
                    S.run("dve", lambda e: e.tensor_tensor(out=x1t[xs][0:m, :], in0=DPp[0:m, :], in1=x1t[xs][0:m, :],
                                                           op=ALU.add),
                          reads=[D_DPp], writes=[D_x1t[xs]])
                    S.run("act", lambda e: e.activation(out=junk[0:m, :], in_=x1t[xs][0:m, :], func=AF.Square,
                                                        accum_out=stat[0:m, 4:5]),
                          reads=[D_x1t[xs]], writes=[D_junk, D_ss])
                    rstd_chain(stat[0:m, 4:5], stat[0:m, 5:6], D_ss, D_rs)
                    S.run("dve", lambda e: e.scalar_tensor_tensor(out=yt[xs][0:m, :], in0=x1t[xs][0:m, :],
                                                                  scalar=stat[0:m, 5:6], in1=gfb[0:m, :],
                                                                  op0=ALU.mult, op1=ALU.mult),
                          reads=[D_x1t[xs], D_rs, D_l], writes=[D_yt[xs]])
                    o0 = s0 - (OWN0 - Q0) + off
                    S.dma("sp", ds_yt[xs], out_d[o0:o0 + m, :], yt[xs][0:m, :], reads=[D_yt[xs]], writes=[D_out])
                    off += m
            S.barrier()
        stCD.close()
    return nc


def _bias_blocks(rpb):
    qc = np.arange(64)
    kc = np.arange(64)
    cs = np.clip(qc - 8, 0, 48)
    inwin = (kc[:, None] >= cs[None, :]) & (kc[:, None] < cs[None, :] + 16)
    coff = np.clip(kc[:, None] - qc[None, :] + 15, 0, 30)
    blk = rpb[:, :, coff]
    return np.where(inwin[None, None], blk, np.float32(NEGV)).astype(np.float32)


def _table(blk, dvals):
    tb = np.full((2, 64, 8, 2, 64), NEGV, np.float32)
    for kl in range(2):
        for ql in range(2):
            d = dvals[kl][ql]
            if d is not None:
                tb[kl, :, :, ql, :] = blk[:, d].transpose(1, 0, 2)
    return tb.reshape(128, 1024)


def _std_tables(blk):
    out = []
    for c in range(5):
        dv = [[None, None], [None, None]]
        for kl in range(2):
            for ql in range(2):
                rel = 2 * c + kl - ql
                if 0 <= rel <= 7:
                    dv[kl][ql] = rel + 3
        out.append(_table(blk, dv))
    return np.stack(out)


def _edge_tables(blk, half):
    R0 = half * 64
    out = []
    for i in sorted(EDGE_PAIRS):
        for kt in EDGE_PAIRS[i]:
            dv = [[None, None], [None, None]]
            for kl in range(2):
                kr = R0 - 5 + 2 * kt + kl
                for ql in range(2):
                    r = min(max(R0 - 1 + 2 * i + ql, 0), 127)
                    rs = min(max(r - 4, 0), 120)
                    if rs <= kr < rs + 8:
                        dv[kl][ql] = kr - r + 7
            out.append(_table(blk, dv))
    return np.stack(out)


def kernel(x, norm1_g, w_in, b_in, rpb, w_na_out, conv_dw_w, conv_dw_b, conv_ln_g, conv_ln_b,
           w_conv_out, w_out, norm2_g, w_up, ffn_dw_w, ffn_dw_b, w_down, norm_f_g):
    f32 = lambda a: np.ascontiguousarray(np.asarray(a, dtype=np.float32))
    x = f32(x)
    b_in0 = f32(b_in)[0]
    rep = lambda v, n=128: np.ascontiguousarray(np.broadcast_to(f32(v)[None, :], (n, v.shape[-1])))
    col = lambda v: f32(v).reshape(-1, 128).T

    blk = _bias_blocks(f32(rpb)[0])
    tstd = _std_tables(blk)
    tedge = [_edge_tables(blk, 0), _edge_tables(blk, 1)]

    common = {
        "g1b": rep(f32(norm1_g)[0]), "g2b": rep(f32(norm2_g)[0]), "gfb": rep(f32(norm_f_g)),
        "bvb": rep(b_in0[1024:1536]), "tstd": tstd, "idn": np.eye(128, dtype=np.float32),
        "w_in": f32(w_in)[0], "w_na": f32(w_na_out)[0], "w_co": f32(w_conv_out)[0], "w_out": f32(w_out)[0],
        "w_up": f32(w_up)[0], "w_dn": f32(w_down)[0],
    }
    pp_base = np.zeros((128, NPP), np.float32)
    pp_base[:, C_BIN:C_BIN + 36] = col(b_in0)
    cw = f32(conv_dw_w)[0]
    pp_base[:, C_CW:C_CW + 124] = cw.reshape(31, 4, 128).transpose(2, 1, 0).reshape(128, 124)
    pp_base[:, C_CB:C_CB + 4] = col(f32(conv_dw_b)[0])
    pp_base[:, C_LG:C_LG + 4] = col(f32(conv_ln_g)[0])
    pp_base[:, C_LB:C_LB + 4] = col(f32(conv_ln_b)[0])
    fw = f32(ffn_dw_w)[0]
    pp_base[:, C_FW:C_FW + 132] = fw.reshape(3, 44, 128).transpose(2, 1, 0).reshape(128, 132)
    pp_base[:, C_FB:C_FB + 44] = col(f32(ffn_dw_b)[0])

    in_maps = []
    for core in range(8):
        b, half = core // 2, core % 2
        t_lo = half * 4096 - OWN0
        xe = np.zeros((TE, 1024), np.float32)
        lo, hi = max(t_lo, 0), min(t_lo + TE, 8192)
        xe[lo - t_lo:hi - t_lo] = x[b, lo:hi]
        valid = np.zeros(TE, np.float32)
        valid[lo - t_lo:hi - t_lo] = 1.0
        pp = pp_base.copy()
        pp[:, C_VT:C_VT + NT] = valid.reshape(NT, 128).T
        m = dict(common)
        m["x_ext"] = xe
        m["pp"] = pp
        m["vmf"] = np.ascontiguousarray(np.broadcast_to(valid[None, :], (128, TE)))
        m["tedge"] = tedge[half]
        in_maps.append(m)

    nc = build_nc()
    res = run_bass_kernel_spmd(nc, in_maps, core_ids=list(range(8)))
    out = np.empty((4, 8192, 1024), np.float32)
    for core in range(8):
        b, half = core // 2, core % 2
        out[b, half * 4096:(half + 1) * 4096] = res.results[core]["out"]
    return out
```
